# Optimizing a Trainium2 kernel written in Bass

```python
import math
import jax
import jax.numpy as jnp
from jax import lax
import numpy as np

D_MODEL = 1024
BATCH = 2
SEQ = 8192
DEPTH = 4

D_A = D_MODEL // 2
GMLP_GROUPS = 4
GMLP_CHUNK = 128
D_B = D_MODEL // 2
CONV_WIDTH = 31
HEAD_DIM = 64
N_HEADS = D_MODEL // 128
D_C = N_HEADS * HEAD_DIM
MOBA_BLOCK = 256
MOBA_TOPK = 3
Q_CHUNK = 128
N_BRANCH = 3
D_BRANCH = 512
W_IN_COLS = 3 * D_A + 3 * D_B + 4 * D_C + N_BRANCH * D_MODEL
ALPHA = (2 * DEPTH) ** 0.25
BETA = (8 * DEPTH) ** -0.25
LN_EPS = 1e-5

kernel_name = "hybrid_gmlp_conformer_moba_deepnorm"


def layer_norm(x, g, b):
    xf = x.astype(jnp.float32)
    mu = jnp.mean(xf, axis=-1, keepdims=True)
    var = jnp.mean(jnp.square(xf - mu), axis=-1, keepdims=True)
    y = (xf - mu) * lax.rsqrt(var + LN_EPS)
    return (y * g + b).astype(x.dtype)


def gmlp_spatial(v, sg_w, sg_b):
    B, S, _ = v.shape
    nc = S // GMLP_CHUNK
    cg = D_A // GMLP_GROUPS
    mask = jnp.tril(jnp.ones((GMLP_CHUNK, GMLP_CHUNK), dtype=bool))
    wm = jnp.where(mask[None], sg_w, jnp.zeros_like(sg_w))
    vc = v.reshape(B, nc, GMLP_CHUNK, GMLP_GROUPS, cg)
    z = jnp.einsum('gts,bnsgc->bntgc', wm, vc) + sg_b.T[None, None, :, :, None]
    return z.reshape(B, S, D_A)


def causal_depthwise_conv(x, w, b):
    C = x.shape[-1]
    y = lax.conv_general_dilated(
        x, w[:, None, :], window_strides=(1,), padding=[(CONV_WIDTH - 1, 0)],
        dimension_numbers=('NWC', 'WIO', 'NWC'), feature_group_count=C)
    return y + b


def moba_attention(q, k, v):
    B, S, H, Dh = q.shape
    nb = -(-S // MOBA_BLOCK)
    pad = nb * MOBA_BLOCK - S
    topk = min(MOBA_TOPK, nb)
    q = q.transpose(0, 2, 1, 3)
    k = jnp.pad(k.transpose(0, 2, 1, 3), ((0, 0), (0, 0), (0, pad), (0, 0)))
    v = jnp.pad(v.transpose(0, 2, 1, 3), ((0, 0), (0, 0), (0, pad), (0, 0)))
    kb = k.reshape(B, H, nb, MOBA_BLOCK, Dh)
    vb = v.reshape(B, H, nb, MOBA_BLOCK, Dh)
    kmean = jnp.mean(kb.astype(jnp.float32), axis=3)
    gate = jnp.einsum('bhsd,bhnd->bhsn', q.astype(jnp.float32), kmean)
    q_blk = jnp.arange(S) // MOBA_BLOCK
    past = jnp.arange(nb)[None, :] < q_blk[:, None]
    gate = jnp.where(past[None, None], gate, -jnp.inf)
    _, sel = lax.top_k(gate, topk)

    nq = S // Q_CHUNK
    q_c = q.reshape(B, H, nq, Q_CHUNK, Dh).transpose(2, 0, 1, 3, 4)
    sel_c = sel.reshape(B, H, nq, Q_CHUNK, topk).transpose(2, 0, 1, 3, 4)
    slopes = 2.0 ** (-(jnp.arange(1, H + 1, dtype=jnp.float32) * 8.0 / H))
    scale = Dh ** -0.5
    bi = jnp.arange(B)[:, None, None, None]
    hi = jnp.arange(H)[None, :, None, None]
    kpos = jnp.arange(MOBA_BLOCK)

    def one_chunk(args):
        qc, selc, c = args
        t = c * Q_CHUNK + jnp.arange(Q_CHUNK)
        blk = (c * Q_CHUNK) // MOBA_BLOCK
        kg = kb[bi, hi, selc]
        vg = vb[bi, hi, selc]
        lp = jnp.einsum('bhqd,bhqnkd->bhqnk', qc, kg).astype(jnp.float32) * scale
        spos = selc[..., None] * MOBA_BLOCK + kpos
        dist_p = (t[None, None, :, None, None] - spos).astype(jnp.float32)
        lp = lp - slopes[None, :, None, None, None] * dist_p
        valid = (jnp.arange(topk) < blk)[None, None, None, :, None]
        lp = jnp.where(valid, lp, -jnp.inf).reshape(B, H, Q_CHUNK, topk * MOBA_BLOCK)
        ko = lax.dynamic_index_in_dim(kb, blk, axis=2, keepdims=False)
        vo = lax.dynamic_index_in_dim(vb, blk, axis=2, keepdims=False)
        lo = jnp.einsum('bhqd,bhkd->bhqk', qc, ko).astype(jnp.float32) * scale
        dist_o = t[:, None] - (blk * MOBA_BLOCK + kpos)[None, :]
        lo = jnp.where(dist_o[None, None] >= 0,
                       lo - slopes[None, :, None, None] * dist_o.astype(jnp.float32)[None, None],
                       -jnp.inf)
        p = jax.nn.softmax(jnp.concatenate([lp, lo], axis=-1), axis=-1).astype(vb.dtype)
        out = jnp.einsum('bhqn,bhqnd->bhqd', p[..., :topk * MOBA_BLOCK],
                         vg.reshape(B, H, Q_CHUNK, topk * MOBA_BLOCK, Dh))
        out = out + jnp.einsum('bhqk,bhkd->bhqd', p[..., topk * MOBA_BLOCK:], vo)
        return out

    out = lax.map(one_chunk, (q_c, sel_c, jnp.arange(nq)))
    return out.transpose(1, 0, 3, 2, 4).reshape(B, S, H * Dh)


def hybrid_layer(x, w_in, sg_w, sg_b, v_ln_g, v_ln_b, conv_w, conv_b, cv_ln_g, cv_ln_b,
                 w_branch, w_out, ln_g, ln_b):
    B, S, D = x.shape
    h = x @ w_in
    cuts = np.cumsum([D_A, D_A, D_A, D_B, D_B, D_B, D_C, D_C, D_C, D_C])
    (a_u, a_v, a_g, b_val, b_glu, b_g, c_q, c_k, c_v, c_g, merge) = jnp.split(h, cuts, axis=-1)
    y_a = a_u * gmlp_spatial(layer_norm(a_v, v_ln_g, v_ln_b), sg_w, sg_b)
    y_a = y_a * jax.nn.silu(a_g)
    glu = b_val * jax.nn.sigmoid(b_glu)
    cv = jax.nn.silu(layer_norm(causal_depthwise_conv(glu, conv_w, conv_b), cv_ln_g, cv_ln_b))
    y_b = cv * jax.nn.silu(b_g)
    att = moba_attention(c_q.reshape(B, S, N_HEADS, HEAD_DIM),
                         c_k.reshape(B, S, N_HEADS, HEAD_DIM),
                         c_v.reshape(B, S, N_HEADS, HEAD_DIM))
    y_c = att * jax.nn.silu(c_g)
    ys = jnp.stack([y_a, y_b, y_c], axis=2)
    branch = jnp.einsum('bsnc,ncd->bsnd', ys, w_branch)
    gates = jax.nn.sigmoid(merge.reshape(B, S, N_BRANCH, D))
    mixed = jnp.sum(gates * branch, axis=2)
    out = mixed @ w_out
    return layer_norm(ALPHA * x + out, ln_g, ln_b)


def setup_inputs(seed: int = 0) -> dict:
    key = jax.random.key(seed)
    ks = jax.random.split(key, 16)
    f32 = jnp.float32
    cg_t = GMLP_CHUNK
    return {
        "x": jax.random.normal(ks[0], (BATCH, SEQ, D_MODEL), f32),
        "w_in": jax.random.normal(ks[1], (DEPTH, D_MODEL, W_IN_COLS), f32) * D_MODEL ** -0.5,
        "sg_w": jax.random.normal(ks[2], (DEPTH, GMLP_GROUPS, cg_t, cg_t), f32) * cg_t ** -0.5,
        "sg_b": 1.0 + 0.01 * jax.random.normal(ks[3], (DEPTH, GMLP_GROUPS, cg_t), f32),
        "v_ln_g": 1.0 + 0.02 * jax.random.normal(ks[4], (DEPTH, D_A), f32),
        "v_ln_b": 0.02 * jax.random.normal(ks[5], (DEPTH, D_A), f32),
        "conv_w": jax.random.normal(ks[6], (DEPTH, CONV_WIDTH, D_B), f32) * CONV_WIDTH ** -0.5,
        "conv_b": 0.02 * jax.random.normal(ks[7], (DEPTH, D_B), f32),
        "cv_ln_g": 1.0 + 0.02 * jax.random.normal(ks[8], (DEPTH, D_B), f32),
        "cv_ln_b": 0.02 * jax.random.normal(ks[9], (DEPTH, D_B), f32),
        "w_branch": jax.random.normal(ks[10], (DEPTH, N_BRANCH, D_BRANCH, D_MODEL), f32) * (D_BRANCH ** -0.5) * BETA,
        "w_out": jax.random.normal(ks[11], (DEPTH, D_MODEL, D_MODEL), f32) * (D_MODEL ** -0.5) * BETA,
        "ln_g": 1.0 + 0.02 * jax.random.normal(ks[12], (DEPTH, D_MODEL), f32),
        "ln_b": 0.02 * jax.random.normal(ks[13], (DEPTH, D_MODEL), f32),
    }


def reference(x, w_in, sg_w, sg_b, v_ln_g, v_ln_b, conv_w, conv_b, cv_ln_g, cv_ln_b,
              w_branch, w_out, ln_g, ln_b):
    for l in range(DEPTH):
        x = hybrid_layer(x, w_in[l], sg_w[l], sg_b[l], v_ln_g[l], v_ln_b[l], conv_w[l], conv_b[l],
                         cv_ln_g[l], cv_ln_b[l], w_branch[l], w_out[l], ln_g[l], ln_b[l])
    return x
```

```python
import contextlib
import numpy as np
import ml_dtypes
import concourse.bass as bass
import concourse.mybir as mybir
from concourse.bass_utils import run_bass_kernel_spmd

F32 = mybir.dt.float32
BF16 = mybir.dt.bfloat16
AF = mybir.ActivationFunctionType
ALU = mybir.AluOpType

DEPTH = 4
ALPHA = (2 * DEPTH) ** 0.25
LN_EPS = 1e-5
BIG = 30000.0
GS = [[r, 7 - r, 8 + r, 15 - r] for r in range(4)]
NSLOT = [16, 32, 48, 64]
SLOT_OFF = [0, 16, 48, 96]
PASS_TILES = [(0, 3), (1, 2)]
ENGS = ("pe", "act", "dve", "pool", "sp")


class Prog:
    def __init__(self, nc):
        self.nc = nc
        self.ops = []
        self.st = {}
        self.streams = {}

    def _recs(self, key):
        if isinstance(key, str):
            key = (key,)
        ent = self.st.setdefault(key[0], {"whole": {"w": None, "r": []}, "sub": {}})
        if len(key) == 1:
            return ent, None
        idx = key[1:]
        if idx not in ent["sub"]:
            ent["sub"][idx] = {"w": ent["whole"]["w"], "r": list(ent["whole"]["r"])}
        return ent, idx

    def op(self, eng, fn, reads=(), writes=(), dma=None, ninc=1):
        i = len(self.ops)
        deps = {}

        def add(j, kind):
            if j is None:
                return
            if j in deps and deps[j] != "war":
                return
            deps[j] = kind

        for key in reads:
            ent, idx = self._recs(key)
            if idx is None:
                add(ent["whole"]["w"], "raw")
                for s in ent["sub"].values():
                    add(s["w"], "raw")
            else:
                add(ent["sub"][idx]["w"], "raw")
                if key[0] == "ps":
                    for r in ent["sub"][idx]["r"]:
                        if self.ops[r]["eng"] != eng:
                            add(r, "rar")
        for key in writes:
            ent, idx = self._recs(key)
            if idx is None:
                add(ent["whole"]["w"], "waw")
                for r in ent["whole"]["r"]:
                    add(r, "war")
                for s in ent["sub"].values():
                    add(s["w"], "waw")
                    for r in s["r"]:
                        add(r, "war")
            else:
                s = ent["sub"][idx]
                add(s["w"], "waw")
                for r in s["r"]:
                    add(r, "war")
        for key in reads:
            ent, idx = self._recs(key)
            if idx is None:
                ent["whole"]["r"].append(i)
                for s in ent["sub"].values():
                    s["r"].append(i)
            else:
                ent["sub"][idx]["r"].append(i)
        for key in writes:
            ent, idx = self._recs(key)
            if idx is None:
                ent["whole"] = {"w": i, "r": []}
                ent["sub"] = {}
            else:
                ent["sub"][idx] = {"w": i, "r": []}
        deps.pop(i, None)
        if dma is not None:
            self.streams.setdefault(dma, 0)
        self.ops.append({"eng": eng, "fn": fn, "deps": deps, "dma": dma, "ninc": ninc,
                         "sig": False, "val": None})
        return i

    def emit(self, final_wait_engine="sp"):
        nc = self.nc
        ops = self.ops
        need = []
        for i, o in enumerate(ops):
            w = []
            for j, kind in o["deps"].items():
                p = ops[j]
                if p["dma"] is None and p["eng"] == o["eng"] and o["dma"] is None:
                    if o["eng"] == "pe":
                        continue
                w.append(j)
                p["sig"] = True
            need.append(w)
        cnt = {e: 0 for e in ENGS}
        scnt = {s: 0 for s in self.streams}
        for o in ops:
            if o["dma"] is not None:
                scnt[o["dma"]] += 16 * o["ninc"]
                o["val"] = scnt[o["dma"]]
            elif o["sig"]:
                cnt[o["eng"]] += 1
                o["val"] = cnt[o["eng"]]
        with contextlib.ExitStack() as es:
            esem = {e: es.enter_context(nc.semaphore("c_" + e)) for e in ENGS}
            ssem = {s: es.enter_context(nc.semaphore("d_" + s)) for s in self.streams}
            block = es.enter_context(nc.Block())
            bmap = {"pe": block.tensor, "act": block.scalar, "dve": block.vector,
                    "pool": block.gpsimd, "sp": block.sync}
            for e in ENGS:
                mine = [(i, o) for i, o in enumerate(ops) if o["eng"] == e]
                if not mine and e != final_wait_engine:
                    continue

                def body(engobj, mine=mine, e=e):
                    waited = {}
                    for i, o in mine:
                        for j in need[i]:
                            p = ops[j]
                            if p["dma"] is not None:
                                k, sem = ("s", p["dma"]), ssem[p["dma"]]
                            else:
                                k, sem = ("e", p["eng"]), esem[p["eng"]]
                            if waited.get(k, 0) >= p["val"]:
                                continue
                            engobj.wait_ge(sem, p["val"])
                            waited[k] = p["val"]
                        r = o["fn"](engobj)
                        if o["dma"] is not None:
                            rl = r if isinstance(r, (list, tuple)) else [r]
                            assert len(rl) == o["ninc"]
                            for ins in rl:
                                ins.then_inc(ssem[o["dma"]], 16)
                        elif o["sig"]:
                            r.then_inc(esem[e], 1)
                    if e == final_wait_engine:
                        for s, v in scnt.items():
                            if v > 0:
                                engobj.wait_ge(ssem[s], v)

                bmap[e](body)


def MM(P, out, lhsT, rhs, start, stop, reads, writes):
    P.op("pe", lambda e: e.matmul(out, lhsT=lhsT, rhs=rhs, start=start, stop=stop), reads, writes)


def TR(P, out, in_, ident, reads, writes):
    P.op("pe", lambda e: e.transpose(out, in_, ident), reads, writes)


def ACT(P, out, in_, func, reads, writes, bias=0.0, scale=1.0):
    P.op("act", lambda e: e.activation(out=out, in_=in_, func=func, bias=bias, scale=scale), reads, writes)


def TT(P, eng, out, in0, in1, op, reads, writes):
    P.op(eng, lambda e: e.tensor_tensor(out=out, in0=in0, in1=in1, op=op), reads, writes)


def TS(P, eng, out, in0, s1, s2, op0, op1, reads, writes):
    if op1 is None:
        P.op(eng, lambda e: e.tensor_scalar(out=out, in0=in0, scalar1=s1, scalar2=None, op0=op0), reads, writes)
    else:
        P.op(eng, lambda e: e.tensor_scalar(out=out, in0=in0, scalar1=s1, scalar2=s2, op0=op0, op1=op1),
             reads, writes)


def STT(P, eng, out, in0, scalar, in1, op0, op1, reads, writes):
    P.op(eng, lambda e: e.scalar_tensor_tensor(out=out, in0=in0, scalar=scalar, in1=in1, op0=op0, op1=op1),
         reads, writes)


def CP(P, eng, out, in_, reads, writes):
    P.op(eng, lambda e: e.tensor_copy(out=out, in_=in_), reads, writes)


def OP1(P, eng, name, out, in_, reads, writes):
    P.op(eng, lambda e: getattr(e, name)(out=out, in_=in_), reads, writes)


def DMA(P, eng, out, in_, reads, writes, stream):
    P.op(eng, lambda e: e.dma_start(out=out, in_=in_), reads, writes, dma=stream)


def DMAS(P, eng, pairs, reads, writes, stream):
    P.op(eng, lambda e: [e.dma_start(out=o, in_=i) for (o, i) in pairs], reads, writes, dma=stream, ninc=len(pairs))


def LOADC(P, stage, dst, src, dims, dkeys, np_=128):
    n = int(np.prod(dims))
    sv = stage[0:np_, 0:n]
    if len(dims) == 2:
        sv = sv.rearrange("p (a b) -> p a b", a=dims[0])
    elif len(dims) == 3:
        sv = sv.rearrange("p (a b c) -> p a b c", a=dims[0], b=dims[1])
    DMA(P, "sp", sv, src, [], ["stage"], "stage")
    CP(P, "pool", dst, sv, ["stage"], dkeys)


def FENCE(P, dummy, old, new):
    P.op("dve", lambda e: e.memset(dummy, 0.0), reads=[], writes=list(old) + list(new) + ["fence_dummy"])


def build_A():
    nc = bass.Bass("TRN2", target_bir_lowering=False)
    Din = lambda n, s, dt=F32: nc.dram_tensor(n, s, dt, kind="ExternalInput").ap()
    Dout = lambda n, s, dt=F32: nc.dram_tensor(n, s, dt, kind="ExternalOutput").ap()
    xT = Din("xT", [8, 128, 2048])
    w_in_t = Din("w_in_t", [16, 128, 8, 512])
    K_loc = Dout("K_loc", [4, 128, 2048], BF16)
    V_loc = Dout("V_loc", [4, 128, 16, 2, 65], BF16)
    km_loc = Dout("km_loc", [128, 4, 8])
    tail_loc = Dout("tail_loc", [128, 4, 4, 30])
    P = Prog(nc)
    with contextlib.ExitStack() as es:
        T = lambda n, s, dt: es.enter_context(nc.sbuf_tensor(n, s, dt))
        xb = T("xb", [128, 8, 2048], BF16)
        wseg = [T("wseg%d" % i, [128, 8, 512], BF16) for i in range(2)]
        Kt = T("Kt", [128, 4, 2048], BF16)
        Vt = T("Vt", [128, 16, 8, 65], BF16)
        km = T("km", [128, 4, 8], F32)
        ks = T("ks", [128, 2], F32)
        bv = T("bv", [128, 4, 4, 30], F32)
        sg = T("sg", [128, 30], F32)
        tl = T("tl", [128, 4, 4, 30], F32)
        ps = [es.enter_context(nc.psum_tensor("ps%d" % i, [128, 512], F32)) for i in range(4)]
        stage = T("stage", [128, 4096], F32)
        for k2 in range(4):
            LOADC(P, stage, xb[:, 2 * k2:2 * k2 + 2, :], xT[2 * k2:2 * k2 + 2].rearrange("k p c -> p k c"),
                  (2, 2048), [("xb", k2)])
        import os
        SK = os.environ.get("A_SKIP", "")
        if "m" not in SK:
            P.op("dve", lambda e: e.memset(Vt[:, :, :, 64:65], 1.0), [], ["Vt"])
        segs = [7, 8, 3, 4]
        bank = 0

        def load(si):
            wk = "wseg%d" % (si % 2)
            LOADC(P, stage, wseg[si % 2][:], w_in_t[segs[si]], (8, 512), [wk])

        load(0)
        load(1)
        w, wk = wseg[0], "wseg0"
        for hp in range(4):
            for t in range(4):
                pb = ps[bank % 4]; pk = ("ps", bank % 4); bank += 1
                for kc in range(8):
                    MM(P, pb[:], w[:, kc, hp * 128:(hp + 1) * 128], xb[:, kc, t * 512:(t + 1) * 512],
                       kc == 0, kc == 7, [wk, "xb"], [pk])
                ACT(P, Kt[:, hp, t * 512:(t + 1) * 512], pb[:], AF.Identity, [pk], [("Kt", hp)])
                if "r" not in SK:
                    P.op("dve", lambda e, pb=pb: e.tensor_reduce(
                        out=ks[:], in_=pb[:].rearrange("p (b k) -> p b k", b=2), axis=mybir.AxisListType.X,
                        op=ALU.add), [pk], ["ks"])
                    TS(P, "dve", km[:, hp, 2 * t:2 * t + 2], ks[:], 1.0 / 256, None, ALU.mult, None, ["ks"], ["km"])
            if "o" not in SK:
                DMA(P, "sp", K_loc[hp], Kt[:, hp, :], [("Kt", hp)], ["K_loc%d" % hp], "Kout")
        if "r" not in SK and "d" not in SK:
            DMA(P, "sp", km_loc, km[:], ["km"], ["km_loc"], "kmout")
        import os
        if os.environ.get("A_STOP") == "k":
            P.emit()
            return nc
        load(2)
        w, wk = wseg[1], "wseg1"
        for ch in range(16):
            pb = ps[bank % 4]; pk = ("ps", bank % 4); bank += 1
            for kc in range(8):
                MM(P, pb[:], xb[:, kc, ch * 128:(ch + 1) * 128], w[:, kc, :], kc == 0, kc == 7,
                   [wk, "xb"], [pk])
            if ch % 2:
                CP(P, "dve", Vt[:, ch, :, 0:64], pb[:].rearrange("p (h d) -> p h d", h=8), [pk], [("Vt", ch)])
            else:
                ACT(P, Vt[:, ch, :, 0:64], pb[:].rearrange("p (h d) -> p h d", h=8), AF.Identity, [pk], [("Vt", ch)])
        for hp in range(4):
            DMA(P, "sp", V_loc[hp], Vt[:, :, 2 * hp:2 * hp + 2, :], ["Vt"], ["V_loc%d" % hp], "Vout")
        if os.environ.get("A_STOP") == "v":
            P.emit()
            return nc
        load(3)
        for si, dst in ((2, "val"), (3, "glu")):
            w, wk = wseg[si % 2], "wseg%d" % (si % 2)
            for cb in range(4):
                for t in range(4):
                    pb = ps[bank % 4]; pk = ("ps", bank % 4); bank += 1
                    c0 = t * 512 + 384
                    for kc in range(8):
                        MM(P, pb[:, 0:128], w[:, kc, cb * 128:(cb + 1) * 128], xb[:, kc, c0:c0 + 128],
                           kc == 0, kc == 7, [wk, "xb"], [pk])
                    if dst == "val":
                        CP(P, "dve", bv[:, cb, t, :], pb[:, 98:128], [pk], [("bv", cb, t)])
                    else:
                        ACT(P, sg[:], pb[:, 98:128], AF.Sigmoid, [pk], ["sg"])
                        TT(P, "dve", tl[:, cb, t, :], bv[:, cb, t, :], sg[:], ALU.mult,
                           [("bv", cb, t), "sg"], [("tl", cb, t)])
        DMA(P, "sp", tail_loc, tl[:], ["tl"], ["tail_loc"], "tlout")
        P.emit()
    return nc


def build_B():
    nc = bass.Bass("TRN2", target_bir_lowering=False)
    Din = lambda n, s, dt=F32: nc.dram_tensor(n, s, dt, kind="ExternalInput").ap()
    xT = Din("xT", [8, 128, 2048])
    w_in_t = Din("w_in_t", [16, 128, 8, 512])
    wm_t = Din("wm_t", [8, 128, 3, 8, 128])
    wb_t = Din("wb_t", [8, 128, 2, 4, 128])
    wb2_t = Din("wb2_t", [8, 64, 8, 128])
    wo_t = Din("wo_t", [128, 8, 1024])
    sgwT = Din("sgwT", [128, 4, 128])
    sgb_bc = Din("sgb_bc", [128, 4, 128])
    vg_bc = Din("vg_bc", [128, 512])
    vb_bc = Din("vb_bc", [128, 512])
    convwT = Din("convwT", [128, 4, 31])
    colp = Din("colp", [128, 3, 4])
    lnp = Din("lnp", [128, 2, 8])
    K_arr = Din("K_arr", [4, 128, 160 * 128], BF16)
    V_arr = Din("V_arr", [4, 128, 160, 2, 65], BF16)
    km_arr = Din("km_arr", [128, 4, 4, 32])
    halo = Din("halo", [128, 4, 4, 30])
    identF_d = Din("identF", [128, 128])
    maskT_d = Din("maskT", [128, 128])
    alibi_d = Din("alibi", [128, 8, 64])
    eaux_d = Din("eaux", [32, 32, 128])
    cmask_d = Din("cmask", [128, 4, 2, 32])
    force_d = Din("force", [128, 4, 2, 32])
    shift_d = Din("shiftc", [128, 4, 8])
    ones_d = Din("onesF", [128, 128])
    xT_out = nc.dram_tensor("xT_out", [8, 128, 2048], F32, kind="ExternalOutput").ap()

    P = Prog(nc)
    import os
    STOP = os.environ.get("B_STOP", "")
    with contextlib.ExitStack() as es:
        T = lambda n, s, dt: es.enter_context(nc.sbuf_tensor("s_" + n, s, dt))
        identF = T("identF", [128, 128], F32)
        onesF = T("onesF", [128, 128], F32)
        maskT = T("maskT", [128, 128], F32)
        tri_bf = T("tri_bf", [128, 128], BF16)
        alibi = T("alibi", [128, 8, 64], F32)
        eaux = T("eaux", [96, 32, 128], BF16)
        cmask = T("cmask", [128, 4, 2, 32], F32)
        force = T("force", [128, 4, 2, 32], F32)
        shiftc = T("shiftc", [128, 4, 8], F32)
        sgwT_s = T("sgwT_s", [128, 4, 128], F32)
        WmT = T("WmT", [128, 4, 128], BF16)
        sgb = T("sgb", [128, 4, 128], F32)
        vg = T("vg", [128, 512], F32)
        vb = T("vb", [128, 512], F32)
        convw = T("convw", [128, 4, 31], F32)
        colp_s = T("colp_s", [128, 3, 4], F32)
        lnp_s = T("lnp_s", [128, 2, 8], F32)
        kmb = T("kmb", [128, 4, 4, 32], BF16)
        dummy = T("dummy", [128, 2], F32)
        epsc = T("epsc", [128, 1], F32)
        P.op("dve", lambda e: e.memset(epsc[:], LN_EPS), [], ["epsc"])
        stage = T("stage", [128, 4096], F32)
        ps = [es.enter_context(nc.psum_tensor("ps%d" % i, [128, 512], F32)) for i in range(8)]
        PK = lambda i: ("ps", i)

        for (dst, src, name) in ((identF, identF_d, "identF"), (onesF, ones_d, "onesF"), (maskT, maskT_d, "maskT"),
                                 (alibi, alibi_d, "alibi"), (cmask, cmask_d, "cmask"), (force, force_d, "force"),
                                 (shiftc, shift_d, "shiftc"), (sgwT_s, sgwT, "sgwT_s"), (sgb, sgb_bc, "sgb"),
                                 (vg, vg_bc, "vg"), (vb, vb_bc, "vb"), (convw, convwT, "convw"),
                                 (colp_s, colp, "colp_s"), (lnp_s, lnp, "lnp_s")):
            DMA(P, "sp", dst[:], src, [], [name], "c_" + name)
        LOADC(P, stage, eaux[0:32], eaux_d, (32, 128), [("eaux", 0)], np_=32)
        sv64 = stage[64:96, 0:4096].rearrange("p (a b) -> p a b", a=32)
        DMA(P, "sp", sv64, eaux_d, [], ["stage"], "stage")
        CP(P, "pool", eaux[64:96], sv64, ["stage"], [("eaux", 1)])
        LOADC(P, stage, kmb[:], km_arr, (4, 4, 32), ["kmb"])
        CP(P, "dve", tri_bf[:], maskT[:], ["maskT"], ["tri_bf"])
        for g in range(4):
            TT(P, "dve", WmT[:, g, :], sgwT_s[:, g, :], maskT[:], ALU.mult, ["sgwT_s", "maskT"], ["WmT"])

        for pidx, tl in enumerate(PASS_TILES):
            sfx = "_p%d" % pidx
            with contextlib.ExitStack() as pes:
                TP = lambda n, s, dt: pes.enter_context(nc.sbuf_tensor("s_" + n + sfx, s, dt))
                xb = TP("xb", [128, 8, 2, 512], BF16)
                wseg = [TP("wseg%d" % i, [128, 8, 512], BF16) for i in range(2)]
                y_a = TP("y_a", [128, 4, 2, 512], BF16)
                y_b = TP("y_b", [128, 4, 2, 512], BF16)
                y_c = TP("y_c", [64, 8, 2, 512], BF16)
                tmp = [TP("tmp%d" % i, [128, 512], F32) for i in range(2)]
                N = lambda s: s + sfx
                glob_new = [N("xb"), N("wseg0"), N("wseg1"), N("y_a"), N("y_b"), N("y_c"), N("tmp0"), N("tmp1")]
                wcount = [0]

                def load_seg(seg):
                    i = wcount[0] % 2
                    wcount[0] += 1
                    LOADC(P, stage, wseg[i][:], w_in_t[seg], (8, 512), [N("wseg%d" % i)])
                    return wseg[i], N("wseg%d" % i)

                def load_xb():
                    for tt in range(2):
                        LOADC(P, stage, xb[:, :, tt, :],
                              xT[:, :, tl[tt] * 512:(tl[tt] + 1) * 512].rearrange("k p c -> p k c"), (8, 512),
                              [(N("xb"), tt)])
                bankc = [0]

                def nb():
                    b = bankc[0] % 2
                    bankc[0] += 1
                    return b

                tmpc = [0]

                def nt():
                    i = tmpc[0] % 2
                    tmpc[0] += 1
                    return tmp[i], N("tmp%d" % i)

                def proj_fm(w, wk, cb, tt, M=128, c0=None):
                    b = nb()
                    c0 = cb * 128 if c0 is None else c0
                    for kc in range(8):
                        MM(P, ps[b][0:M, :], w[:, kc, c0:c0 + M], xb[:, kc, tt, :], kc == 0, kc == 7,
                           [wk, (N("xb"), tt)], [PK(b)])
                    return b

                with contextlib.ExitStack() as ph:
                    T1 = lambda n, s, dt: ph.enter_context(nc.sbuf_tensor("s_" + n + sfx, s, dt))
                    zt = T1("zt", [128, 4, 2, 512], F32)
                    vLN = T1("vLN", [128, 8, 512], BF16)
                    glu = T1("glu", [128, 4, 2, 542], BF16)
                    diag = [T1("diag%d" % i, [128, 31, 128], BF16) for i in range(2)]
                    cvraw = T1("cvraw", [128, 4, 2, 512], F32)
                    st6 = T1("st6", [128, 6], F32)
                    mv = T1("mv", [128, 2], F32)
                    rstd = T1("rstd", [128, 1], F32)
                    mS = T1("mS", [128, 512], F32)
                    rS = T1("rS", [128, 512], F32)
                    new1 = [N(s_) for s_ in ("zt", "vLN", "glu", "diag0", "diag1", "cvraw", "st6", "mv", "rstd",
                                             "mS", "rS")]
                    FENCE(P, dummy[:, 0:1], [], glob_new + new1)
                    load_xb()
                    for tt in range(2):
                        LOADC(P, stage, glu[:, :, tt, 0:30], halo[:, :, tl[tt], :], (4, 30),
                              [(N("glu"), cb, tt, "h") for cb in range(4)])
                    w, wk = load_seg(1)
                    wu, wuk = load_seg(0)
                    for tt in range(2):
                        for ci in range(4):
                            b = nb()
                            ch = tt * 4 + ci
                            for kc in range(8):
                                MM(P, ps[b][:], xb[:, kc, tt, ci * 128:(ci + 1) * 128], w[:, kc, :], kc == 0, kc == 7,
                                   [wk, (N("xb"), tt)], [PK(b)])
                            OP1(P, "dve", "bn_stats", st6[:], ps[b][:], [PK(b)], [N("st6")])
                            OP1(P, "dve", "bn_aggr", mv[:], st6[:], [N("st6")], [N("mv")])
                            ACT(P, rstd[:], mv[:, 1:2], AF.Sqrt, [N("mv"), "epsc"], [N("rstd")], bias=epsc[:])
                            OP1(P, "dve", "reciprocal", rstd[:], rstd[:], [N("rstd")], [N("rstd")])
                            t_, tk = nt()
                            TS(P, "dve", t_[:], ps[b][:], mv[:, 0:1], rstd[:], ALU.subtract, ALU.mult,
                               [PK(b), N("mv"), N("rstd")], [tk])
                            TT(P, "pool", t_[:], t_[:], vg[:], ALU.mult, [tk, "vg"], [tk])
                            TT(P, "pool", vLN[:, ch, :], t_[:], vb[:], ALU.add, [tk, "vb"], [(N("vLN"), ch)])
                    if STOP == "av":
                        P.emit()
                        return nc
                    for tt in range(2):
                        for g in range(4):
                            b = 2 + (g % 2)
                            for ci in range(4):
                                ch = tt * 4 + ci
                                MM(P, ps[b][:, ci * 128:(ci + 1) * 128], vLN[:, ch, g * 128:(g + 1) * 128],
                                   WmT[:, g, :], True, True, [(N("vLN"), ch), "WmT"], [PK(b)])
                            for ci in range(4):
                                TT(P, "dve", zt[:, g, tt, ci * 128:(ci + 1) * 128], ps[b][:, ci * 128:(ci + 1) * 128],
                                   sgb[:, g, :], ALU.add, [PK(b), "sgb"], [(N("zt"), g, tt)])
                    if STOP == "sp":
                        P.emit()
                        return nc
                    w, wk = wu, wuk
                    wg, wgk = load_seg(2)
                    for cb in range(4):
                        for tt in range(2):
                            b = proj_fm(w, wk, cb, tt)
                            TT(P, "dve", zt[:, cb, tt, :], ps[b][:], zt[:, cb, tt, :], ALU.mult,
                               [PK(b), (N("zt"), cb, tt)], [(N("zt"), cb, tt)])
                    w, wk = wg, wgk
                    wv, wvk = load_seg(3)
                    for cb in range(4):
                        for tt in range(2):
                            b = proj_fm(w, wk, cb, tt)
                            t_, tk = nt()
                            ACT(P, t_[:], ps[b][:], AF.Silu, [PK(b)], [tk])
                            TT(P, "dve", y_a[:, cb, tt, :], zt[:, cb, tt, :], t_[:], ALU.mult,
                               [tk, (N("zt"), cb, tt)], [(N("y_a"), cb, tt)])
                    if STOP == "ya":
                        P.emit()
                        return nc
                    w, wk = wv, wvk
                    wl, wlk = load_seg(4)
                    for cb in range(4):
                        for tt in range(2):
                            b = proj_fm(w, wk, cb, tt)
                            CP(P, "dve", zt[:, cb, tt, :], ps[b][:], [PK(b)], [(N("zt"), cb, tt)])
                    w, wk = wl, wlk
                    wbg, wbgk = load_seg(5)
                    for cb in range(4):
                        for tt in range(2):
                            b = proj_fm(w, wk, cb, tt)
                            t_, tk = nt()
                            ACT(P, t_[:], ps[b][:], AF.Sigmoid, [PK(b)], [tk])
                            TT(P, "dve", glu[:, cb, tt, 30:542], zt[:, cb, tt, :], t_[:], ALU.mult,
                               [tk, (N("zt"), cb, tt)], [(N("glu"), cb, tt, "m")])
                    if STOP == "glu":
                        P.emit()
                        return nc
                    for cb in range(4):
                        dg, dgk = diag[cb % 2], N("diag%d" % (cb % 2))
                        for j in range(31):
                            TS(P, "pool", dg[:, j, :], identF[:], convw[:, cb, j:j + 1], None, ALU.mult, None,
                               ["identF", "convw"], [dgk])
                        for tt in range(2):
                            b = 2 + (tt % 2)
                            for j in range(31):
                                MM(P, ps[b][:], dg[:, j, :], glu[:, cb, tt, j:j + 512], j == 0, j == 30,
                                   [dgk, (N("glu"), cb, tt, "h"), (N("glu"), cb, tt, "m")], [PK(b)])
                            ACT(P, cvraw[:, cb, tt, :], ps[b][:], AF.Identity, [PK(b), "colp_s"],
                                [(N("cvraw"), cb, tt)], bias=colp_s[:, 0, cb:cb + 1])
                    if STOP == "conv":
                        P.emit()
                        return nc
                    for tt in range(2):
                        for cb in range(4):
                            t_, tk = nt()
                            ACT(P, t_[:], cvraw[:, cb, tt, :], AF.Square, [(N("cvraw"), cb, tt)], [tk])
                            MM(P, ps[4][:], onesF[:], cvraw[:, cb, tt, :], cb == 0, cb == 3,
                               ["onesF", (N("cvraw"), cb, tt)], [PK(4)])
                            MM(P, ps[5][:], onesF[:], t_[:], cb == 0, cb == 3, ["onesF", tk], [PK(5)])
                        TS(P, "dve", mS[:], ps[4][:], 1.0 / 512, None, ALU.mult, None, [PK(4)], [N("mS")])
                        t_, tk = nt()
                        TT(P, "dve", t_[:], mS[:], mS[:], ALU.mult, [N("mS")], [tk])
                        STT(P, "dve", rS[:], ps[5][:], 1.0 / 512, t_[:], ALU.mult, ALU.subtract, [PK(5), tk], [N("rS")])
                        ACT(P, rS[:], rS[:], AF.Sqrt, [N("rS"), "epsc"], [N("rS")], bias=epsc[:])
                        OP1(P, "dve", "reciprocal", rS[:], rS[:], [N("rS")], [N("rS")])
                        for cb in range(4):
                            t_, tk = nt()
                            TT(P, "dve", t_[:], cvraw[:, cb, tt, :], mS[:], ALU.subtract,
                               [(N("cvraw"), cb, tt), N("mS")], [tk])
                            TT(P, "pool", t_[:], t_[:], rS[:], ALU.mult, [tk, N("rS")], [tk])
                            ACT(P, zt[:, cb, tt, :], t_[:], AF.Silu, [tk, "colp_s"], [(N("zt"), cb, tt)],
                                bias=colp_s[:, 2, cb:cb + 1], scale=colp_s[:, 1, cb:cb + 1])
                    if STOP == "cvln":
                        P.emit()
                        return nc
                    w, wk = wbg, wbgk
                    wq, wqk = load_seg(6)
                    for cb in range(4):
                        for tt in range(2):
                            b = proj_fm(w, wk, cb, tt)
                            t_, tk = nt()
                            ACT(P, t_[:], ps[b][:], AF.Silu, [PK(b)], [tk])
                            TT(P, "dve", y_b[:, cb, tt, :], zt[:, cb, tt, :], t_[:], ALU.mult,
                               [tk, (N("zt"), cb, tt)], [(N("y_b"), cb, tt)])
                    old1 = new1

                if STOP == "p1":
                    P.emit()
                    return nc
                with contextlib.ExitStack() as ph:
                    T2 = lambda n, s, dt: ph.enter_context(nc.sbuf_tensor("s_" + n + sfx, s, dt))
                    QT = T2("QT", [128, 4, 2, 512], BF16)
                    pen = T2("pen", [128, 8, 8, 32], F32)
                    gsb = T2("gsb", [128, 8, 32], F32)
                    top8 = T2("top8", [128, 8, 8], F32)
                    Qaux = [T2("Qaux%d" % i, [96, 2, 1024], BF16) for i in range(2)]
                    qtmp = T2("qtmp", [32, 1024], BF16)
                    Kb = [T2("Kb%d" % i, [128, NSLOT[tl[i]] * 128], BF16) for i in range(2)]
                    Vb = [T2("Vb%d" % i, [128, NSLOT[tl[i]], 2, 65], BF16) for i in range(2)]
                    PT = [T2("PT%d" % i, [128, 512], BF16) for i in range(4)]
                    wcg = T2("wcg", [128, 8, 512], BF16)
                    sgc = T2("sgc", [64, 512], F32)
                    rec = T2("rec", [65, 512], F32)
                    tat = T2("tat", [64, 512], F32)
                    new2 = [N(s) for s in ("QT", "pen", "gsb", "top8", "Qaux0", "Qaux1", "qtmp", "Kb0", "Kb1", "Vb0", "Vb1",
                                           "PT0", "PT1", "PT2", "PT3", "wcg", "sgc", "rec", "tat")]
                    FENCE(P, dummy[:, 0:1], old1, new2)
                    LOADC(P, stage, wcg[:], w_in_t[9], (8, 512), [N("wcg")])
                    w, wk = wq, wqk
                    for hp in range(4):
                        for tt in range(2):
                            b = proj_fm(w, wk, hp, tt)
                            TS(P, "dve", QT[:, hp, tt, :], ps[b][:], 0.125, None, ALU.mult, None, [PK(b)],
                               [(N("QT"), hp, tt)])
                    if STOP == "q":
                        P.emit()
                        return nc
                    for tt in range(2):
                        ti = tl[tt]
                        for ci in range(4):
                            ch = tt * 4 + ci
                            hf = ci // 2
                            for h2 in range(2):
                                for hp in range(4):
                                    h = hp * 2 + h2
                                    MM(P, ps[2 + h2][:, h * 32:(h + 1) * 32],
                                       QT[h2 * 64:(h2 + 1) * 64, hp, tt, ci * 128:(ci + 1) * 128],
                                       kmb[h2 * 64:(h2 + 1) * 64, hp, ti, :], True, True,
                                       [(N("QT"), hp, tt), "kmb"], [PK(2 + h2)])
                            for h in range(8):
                                TT(P, "dve", gsb[:, h, :], ps[2 + h % 2][:, h * 32:(h + 1) * 32], cmask[:, ti, hf, :],
                                   ALU.add, [PK(2 + h % 2), "cmask"], [(N("gsb"), h)])
                                OP1(P, "dve", "max", top8[:, h, :], gsb[:, h, :], [(N("gsb"), h)], [(N("top8"), h)])
                                TS(P, "dve", gsb[:, h, :], gsb[:, h, :], top8[:, h, 2:3], -BIG, ALU.is_lt, ALU.mult,
                                   [(N("gsb"), h), (N("top8"), h)], [(N("gsb"), h)])
                                STT(P, "dve", pen[:, ch, h, :], gsb[:, h, :], shiftc[:, ci, h:h + 1],
                                    force[:, ti, hf, :], ALU.add, ALU.add, [(N("gsb"), h), "shiftc", "force"],
                                    [(N("pen"), ch, h)])
                    if STOP == "gate":
                        P.emit()
                        return nc
                    kvc = [0]
                    pending = {}

                    def load_kv(hp, tt):
                        i = kvc[0] % 2
                        kvc[0] += 1
                        ti = tl[tt]
                        n = NSLOT[ti]
                        o = SLOT_OFF[ti]
                        DMA(P, "sp", Kb[i][:, 0:n * 128], K_arr[hp, :, o * 128:(o + n) * 128], [], [N("Kb%d" % i)],
                            N("Kb%d" % i))
                        DMA(P, "sp", Vb[i][:, 0:n, :, :], V_arr[hp, :, o:o + n, :, :], [], [N("Vb%d" % i)],
                            N("Vb%d" % i))
                        pending[(hp, tt)] = i

                    def build_qaux(hp):
                        qa, qk = Qaux[hp % 2], N("Qaux%d" % (hp % 2))
                        for h2 in range(2):
                            h = hp * 2 + h2
                            for tt in range(2):
                                for ci in range(4):
                                    ch = tt * 4 + ci
                                    TR(P, ps[2][0:32, ci * 128:(ci + 1) * 128], pen[:, ch, h, :], identF[:],
                                       [(N("pen"), ch, h), "identF"], [PK(2)])
                                if h2 == 0:
                                    CP(P, "dve", qa[0:32, 0, tt * 512:(tt + 1) * 512], ps[2][0:32, :], [PK(2)],
                                       [(qk, 0, tt)])
                                else:
                                    CP(P, "dve", qtmp[:, tt * 512:(tt + 1) * 512], ps[2][0:32, :], [PK(2)],
                                       [(N("qtmp"), tt)])
                            if h2 == 1:
                                DMA(P, "sp", qa[64:96, 1, :], qtmp[:, :], [N("qtmp")], [(qk, 1, 0), (qk, 1, 1)],
                                    N("qauxd%d" % (hp % 2)))

                    seq = [(hp, tt) for hp in range(4) for tt in range(2)]
                    load_kv(*seq[0])
                    build_qaux(0)
                    sb = [0]
                    pc = [0]
                    accc = [0]
                    for si, (hp, tt) in enumerate(seq):
                        if si + 1 < len(seq):
                            load_kv(*seq[si + 1])
                        if tt == 1 and hp + 1 < 4:
                            build_qaux(hp + 1)
                        i = pending[(hp, tt)]
                        kb_, kk = Kb[i], N("Kb%d" % i)
                        vb_, vk = Vb[i], N("Vb%d" % i)
                        qa, qk = Qaux[hp % 2], N("Qaux%d" % (hp % 2))
                        ti = tl[tt]
                        n = NSLOT[ti]
                        for h2 in range(2):
                            h = hp * 2 + h2
                            acc = 6 + (accc[0] % 2)
                            accc[0] += 1
                            units = list(range(n - 1, -1, -1))
                            staged = []

                            def qk_unit(rel):
                                c = 3 - rel if rel <= 3 else 0
                                c0 = 128 * c
                                sbank = 3 + (sb[0] % 3)
                                sb[0] += 1
                                MM(P, ps[sbank][:, c0:512], kb_[h2 * 64:(h2 + 1) * 64, rel * 128:(rel + 1) * 128],
                                   QT[h2 * 64:(h2 + 1) * 64, hp, tt, c0:512], True, False,
                                   [kk, (N("QT"), hp, tt)], [PK(sbank)])
                                a0 = 64 * h2
                                MM(P, ps[sbank][:, c0:512], eaux[a0:a0 + 32, rel // 2, :],
                                   qa[a0:a0 + 32, h2, tt * 512 + c0:tt * 512 + 512],
                                   False, True, [("eaux", h2), (qk, h2, tt)], [PK(sbank)])
                                pi = pc[0] % 4
                                pc[0] += 1
                                ACT(P, PT[pi][:, c0:512], ps[sbank][:, c0:512], AF.Exp, [PK(sbank), "alibi"],
                                    [N("PT%d" % pi)], bias=alibi[:, h, rel:rel + 1])
                                if rel <= 3:
                                    TT(P, "pool", PT[pi][:, c0:c0 + 128], PT[pi][:, c0:c0 + 128], tri_bf[:], ALU.mult,
                                       [N("PT%d" % pi), "tri_bf"], [N("PT%d" % pi)])
                                return (rel, pi, c0)

                            def pv_unit(u, first, last):
                                rel, pi, c0 = u
                                MM(P, ps[acc][0:65, c0:512], vb_[:, rel, h2, :], PT[pi][:, c0:512], first, last,
                                   [vk, N("PT%d" % pi)], [PK(acc)])

                            done = 0
                            for ui, rel in enumerate(units):
                                staged.append(qk_unit(rel))
                                if len(staged) > 2:
                                    pv_unit(staged.pop(0), done == 0, False)
                                    done += 1
                            while staged:
                                u = staged.pop(0)
                                pv_unit(u, done == 0, len(staged) == 0)
                                done += 1
                            for kc in range(8):
                                MM(P, ps[0][0:64, :], wcg[:, kc, h * 64:(h + 1) * 64], xb[:, kc, tt, :], kc == 0, kc == 7,
                                   [N("wcg"), (N("xb"), tt)], [PK(0)])
                            ACT(P, sgc[:], ps[0][0:64, :], AF.Silu, [PK(0)], [N("sgc")])
                            OP1(P, "dve", "reciprocal", rec[64:65, :], ps[acc][64:65, :], [PK(acc)], [N("rec")])
                            MM(P, ps[1][0:64, :], onesF[64:65, 0:64], rec[64:65, :], True, True, ["onesF", N("rec")],
                               [PK(1)])
                            TT(P, "dve", tat[:], ps[acc][0:64, :], sgc[:], ALU.mult, [PK(acc), N("sgc")], [N("tat")])
                            TT(P, "dve", y_c[:, h, tt, :], tat[:], ps[1][0:64, :], ALU.mult, [N("tat"), PK(1)],
                               [(N("y_c"), h, tt)])
                    old2 = new2

                if STOP == "p2":
                    P.emit()
                    return nc
                with contextlib.ExitStack() as ph:
                    T3 = lambda n, s, dt: ph.enter_context(nc.sbuf_tensor("s_" + n + sfx, s, dt))
                    mixb = T3("mixb", [128, 8, 2, 512], BF16)
                    wm = [T3("wm%d" % i, [128, 3, 8, 128], BF16) for i in range(2)]
                    wbr = [T3("wbr%d" % i, [128, 2, 4, 128], BF16) for i in range(2)]
                    wb2 = [T3("wb2%d" % i, [64, 8, 128], BF16) for i in range(2)]
                    wo = T3("wo", [128, 8, 1024], BF16)
                    xres = T3("xres", [128, 8, 512], F32)
                    rr = T3("rr", [128, 8, 512], F32)
                    mix = T3("mix", [128, 512], F32)
                    t2 = T3("t2", [128, 512], F32)
                    mS = T3("mS3", [128, 512], F32)
                    rS = T3("rS3", [128, 512], F32)
                    new3 = [N(s) for s in ("mixb", "wm0", "wm1", "wbr0", "wbr1", "wb20", "wb21", "wo", "xres", "rr",
                                           "mix", "t2", "mS3", "rS3")]
                    FENCE(P, dummy[:, 0:1], old2, new3)
                    for k2 in range(2):
                        LOADC(P, stage, wo[:, 4 * k2:4 * k2 + 4, :], wo_t[:, 4 * k2:4 * k2 + 4, :], (4, 1024),
                              [(N("wo"), k2)])

                    def load_dm(dm):
                        i = dm % 2
                        LOADC(P, stage, wm[i][:], wm_t[dm], (3, 8, 128), [N("wm%d" % i)])
                        LOADC(P, stage, wbr[i][:], wb_t[dm], (2, 4, 128), [N("wbr%d" % i)])
                        LOADC(P, stage, wb2[i][:], wb2_t[dm], (8, 128), [N("wb2%d" % i)], np_=64)

                    load_dm(0)
                    for dm in range(8):
                        if dm + 1 < 8:
                            load_dm(dm + 1)
                        i = dm % 2
                        for tt in range(2):
                            for n in range(3):
                                bm = nb()
                                for kc in range(8):
                                    MM(P, ps[bm][:], wm[i][:, n, kc, :], xb[:, kc, tt, :], kc == 0, kc == 7,
                                       [N("wm%d" % i), (N("xb"), tt)], [PK(bm)])
                                t_, tk = nt()
                                ACT(P, t_[:], ps[bm][:], AF.Sigmoid, [PK(bm)], [tk])
                                bb = 2 + (n % 2)
                                if n < 2:
                                    ysrc, yk = (y_a, N("y_a")) if n == 0 else (y_b, N("y_b"))
                                    for kc in range(4):
                                        MM(P, ps[bb][:], wbr[i][:, n, kc, :], ysrc[:, kc, tt, :], kc == 0, kc == 3,
                                           [N("wbr%d" % i), (yk, kc, tt)], [PK(bb)])
                                else:
                                    for h in range(8):
                                        MM(P, ps[bb][:], wb2[i][:, h, :], y_c[:, h, tt, :], h == 0, h == 7,
                                           [N("wb2%d" % i), (N("y_c"), h, tt)], [PK(bb)])
                                if n == 0:
                                    TT(P, "dve", mix[:], t_[:], ps[bb][:], ALU.mult, [tk, PK(bb)], [N("mix")])
                                elif n == 1:
                                    TT(P, "dve", t2[:], t_[:], ps[bb][:], ALU.mult, [tk, PK(bb)], [N("t2")])
                                    TT(P, "pool", mix[:], mix[:], t2[:], ALU.add, [N("mix"), N("t2")], [N("mix")])
                                else:
                                    TT(P, "dve", t2[:], t_[:], ps[bb][:], ALU.mult, [tk, PK(bb)], [N("t2")])
                                    TT(P, "pool", mixb[:, dm, tt, :], mix[:], t2[:], ALU.add, [N("mix"), N("t2")],
                                       [(N("mixb"), dm, tt)])
                    if STOP == "mrg":
                        P.emit()
                        return nc
                    for tt in range(2):
                        c0 = tl[tt] * 512
                        DMA(P, "sp", xres[:], xT[:, :, c0:c0 + 512].rearrange("k p c -> p k c"), [], [N("xres")],
                            N("xres"))
                        for dmo in range(8):
                            b = 4 + (dmo % 2)
                            for kc in range(8):
                                MM(P, ps[b][:], wo[:, kc, dmo * 128:(dmo + 1) * 128], mixb[:, kc, tt, :], kc == 0, kc == 7,
                                   [N("wo"), (N("mixb"), kc, tt)], [PK(b)])
                            STT(P, "dve", rr[:, dmo, :], xres[:, dmo, :], ALPHA, ps[b][:], ALU.mult, ALU.add,
                                [N("xres"), PK(b)], [(N("rr"), dmo)])
                            t_, tk = nt()
                            ACT(P, t_[:], rr[:, dmo, :], AF.Square, [(N("rr"), dmo)], [tk])
                            MM(P, ps[6][:], onesF[:], rr[:, dmo, :], dmo == 0, dmo == 7, ["onesF", (N("rr"), dmo)],
                               [PK(6)])
                            MM(P, ps[7][:], onesF[:], t_[:], dmo == 0, dmo == 7, ["onesF", tk], [PK(7)])
                        TS(P, "dve", mS[:], ps[6][:], 1.0 / 1024, None, ALU.mult, None, [PK(6)], [N("mS3")])
                        TT(P, "dve", t2[:], mS[:], mS[:], ALU.mult, [N("mS3")], [N("t2")])
                        STT(P, "dve", rS[:], ps[7][:], 1.0 / 1024, t2[:], ALU.mult, ALU.subtract, [PK(7), N("t2")],
                            [N("rS3")])
                        ACT(P, rS[:], rS[:], AF.Sqrt, [N("rS3"), "epsc"], [N("rS3")], bias=epsc[:])
                        OP1(P, "dve", "reciprocal", rS[:], rS[:], [N("rS3")], [N("rS3")])
                        for dmo in range(8):
                            TT(P, "dve", rr[:, dmo, :], rr[:, dmo, :], mS[:], ALU.subtract, [(N("rr"), dmo), N("mS3")],
                               [(N("rr"), dmo)])
                            TT(P, "pool", rr[:, dmo, :], rr[:, dmo, :], rS[:], ALU.mult, [(N("rr"), dmo), N("rS3")],
                               [(N("rr"), dmo)])
                            ACT(P, rr[:, dmo, :], rr[:, dmo, :], AF.Identity, [(N("rr"), dmo), "lnp_s"], [(N("rr"), dmo)],
                                bias=lnp_s[:, 1, dmo:dmo + 1], scale=lnp_s[:, 0, dmo:dmo + 1])
                            DMA(P, "sp", xT_out[dmo, :, c0:c0 + 512], rr[:, dmo, :], [(N("rr"), dmo)],
                                ["xT_out_%d_%d" % (dmo, tl[tt])], "xout%d" % dmo)
                    old3 = new3
                old_glob = glob_new
            FENCE(P, dummy[:, 1:2], old3 + old_glob, ["passdone%d" % pidx])
        P.emit()
    return nc


_CACHE = {}


def _get(name, fn):
    if name not in _CACHE:
        _CACHE[name] = fn()
    return _CACHE[name]


def _consts():
    p = np.arange(128)
    identF = np.eye(128, dtype=np.float32)
    maskT = (p[:, None] <= p[None, :]).astype(np.float32)
    slopes = 2.0 ** (-(np.arange(1, 9, dtype=np.float64)))
    rel = np.arange(64)
    alibi = (slopes[None, :, None] * (128.0 * (3 - rel)[None, None, :] + p[:, None, None])).astype(np.float32)
    eaux = np.zeros((32, 32, 128), np.float32)
    for j in range(32):
        eaux[j, j, :] = 1.0
    shiftc = np.zeros((128, 4, 8), np.float32)
    for ci in range(4):
        shiftc[:, ci, :] = -(slopes[None, :] * (128.0 * ci + p[:, None]))
    ones = np.ones((128, 128), np.float32)
    per_rank = []
    for r in range(4):
        cm = np.zeros((4, 2, 32), np.float32)
        fo = np.zeros((4, 2, 32), np.float32)
        for i, G in enumerate(GS[r]):
            for hf in range(2):
                jr_own = 1 - hf
                for jr in range(32):
                    elig = (jr > jr_own) and (jr <= 2 * G + 1)
                    if elig:
                        cm[i, hf, jr] = 0.0; fo[i, hf, jr] = 0.0
                    elif jr == jr_own:
                        cm[i, hf, jr] = -2e30; fo[i, hf, jr] = BIG
                    else:
                        cm[i, hf, jr] = -1e30; fo[i, hf, jr] = -BIG
        per_rank.append((np.broadcast_to(cm, (128, 4, 2, 32)).copy(), np.broadcast_to(fo, (128, 4, 2, 32)).copy()))
    return dict(identF=identF, maskT=maskT, alibi=alibi, eaux=eaux, shiftc=shiftc, onesF=ones), per_rank


def shard_x(x):
    xT = []
    for c in range(8):
        b, r = c // 4, c % 4
        cols = np.concatenate([x[b, G * 512:(G + 1) * 512, :] for G in GS[r]], axis=0)
        xT.append(np.ascontiguousarray(cols.T.reshape(8, 128, 2048)))
    return xT


def layer_weights(l, w_in, sg_w, sg_b, v_ln_g, v_ln_b, conv_w, conv_b, cv_ln_g, cv_ln_b, w_branch, w_out, ln_g, ln_b):
    f = np.float32
    wi = np.asarray(w_in[l], f)
    d = {}
    d["w_in_t"] = np.ascontiguousarray(wi.reshape(8, 128, 16, 512).transpose(2, 1, 0, 3))
    wmrg = wi[:, 5120:].reshape(8, 128, 3, 8, 128)
    d["wm_t"] = np.ascontiguousarray(wmrg.transpose(3, 1, 2, 0, 4))
    wb = np.asarray(w_branch[l], f)
    d["wb_t"] = np.ascontiguousarray(wb[0:2].reshape(2, 4, 128, 8, 128).transpose(3, 2, 0, 1, 4))
    d["wb2_t"] = np.ascontiguousarray(wb[2].reshape(8, 64, 8, 128).transpose(2, 1, 0, 3))
    d["wo_t"] = np.ascontiguousarray(np.asarray(w_out[l], f).reshape(8, 128, 1024).transpose(1, 0, 2))
    d["sgwT"] = np.ascontiguousarray(np.asarray(sg_w[l], f).transpose(2, 0, 1))
    d["sgb_bc"] = np.ascontiguousarray(np.broadcast_to(np.asarray(sg_b[l], f)[None], (128, 4, 128)))
    d["vg_bc"] = np.ascontiguousarray(np.broadcast_to(np.asarray(v_ln_g[l], f)[None], (128, 512)))
    d["vb_bc"] = np.ascontiguousarray(np.broadcast_to(np.asarray(v_ln_b[l], f)[None], (128, 512)))
    d["convwT"] = np.ascontiguousarray(np.asarray(conv_w[l], f).reshape(31, 4, 128).transpose(2, 1, 0))
    d["colp"] = np.ascontiguousarray(np.stack([np.asarray(a[l], f).reshape(4, 128).T
                                               for a in (conv_b, cv_ln_g, cv_ln_b)], axis=1))
    d["lnp"] = np.ascontiguousarray(np.stack([np.asarray(a[l], f).reshape(8, 128).T for a in (ln_g, ln_b)], axis=1))
    return d


def exchange(resA):
    f = np.float32
    outs = []
    for b in range(2):
        Kg = np.zeros((4, 128, 64, 128), ml_dtypes.bfloat16)
        Vg = np.zeros((4, 128, 64, 2, 65), ml_dtypes.bfloat16)
        kmg = np.zeros((128, 4, 32), f)
        tlg = np.zeros((128, 4, 16, 30), f)
        for r in range(4):
            ra = resA[b * 4 + r]
            Kl = np.asarray(ra["K_loc"]).reshape(4, 128, 4, 4, 128)
            Vl = np.asarray(ra["V_loc"]).reshape(4, 128, 4, 4, 2, 65)
            for i, G in enumerate(GS[r]):
                Kg[:, :, 4 * G:4 * G + 4, :] = Kl[:, :, i]
                Vg[:, :, 4 * G:4 * G + 4] = Vl[:, :, i]
                kmg[:, :, 2 * G:2 * G + 2] = np.asarray(ra["km_loc"])[:, :, 2 * i:2 * i + 2]
                tlg[:, :, G, :] = np.asarray(ra["tail_loc"])[:, :, i, :]
        for r in range(4):
            K_arr = np.zeros((4, 128, 160, 128), ml_dtypes.bfloat16)
            V_arr = np.zeros((4, 128, 160, 2, 65), ml_dtypes.bfloat16)
            km_arr = np.zeros((128, 4, 4, 32), f)
            halo = np.zeros((128, 4, 4, 30), f)
            for i, G in enumerate(GS[r]):
                nv = 4 * G + 4
                idx = (4 * G + 3) - np.arange(nv)
                K_arr[:, :, SLOT_OFF[i]:SLOT_OFF[i] + nv] = Kg[:, :, idx]
                V_arr[:, :, SLOT_OFF[i]:SLOT_OFF[i] + nv] = Vg[:, :, idx]
                nj = 2 * G + 2
                km_arr[:, :, i, 0:nj] = kmg[:, :, (2 * G + 1) - np.arange(nj)]
                if G > 0:
                    halo[:, :, i, :] = tlg[:, :, G - 1, :]
            outs.append({"K_arr": K_arr.reshape(4, 128, 160 * 128), "V_arr": V_arr, "km_arr": km_arr, "halo": halo})
    return outs


def unshard(xT):
    out = np.zeros((2, 8192, 1024), np.float32)
    for c in range(8):
        b, r = c // 4, c % 4
        cols = np.asarray(xT[c], np.float32).reshape(1024, 2048).T
        for i, G in enumerate(GS[r]):
            out[b, G * 512:(G + 1) * 512, :] = cols[i * 512:(i + 1) * 512]
    return out


def kernel(x, w_in, sg_w, sg_b, v_ln_g, v_ln_b, conv_w, conv_b, cv_ln_g, cv_ln_b, w_branch, w_out, ln_g, ln_b):
    x = np.asarray(x, np.float32)
    consts, per_rank = _consts()
    ncA = _get("A", build_A)
    ncB = _get("B", build_B)
    xT = shard_x(x)
    for l in range(DEPTH):
        lw = layer_weights(l, w_in, sg_w, sg_b, v_ln_g, v_ln_b, conv_w, conv_b, cv_ln_g, cv_ln_b, w_branch, w_out,
                           ln_g, ln_b)
        resA = run_bass_kernel_spmd(ncA, [{"xT": xT[c], "w_in_t": lw["w_in_t"]} for c in range(8)],
                                    core_ids=list(range(8))).results
        ex = exchange(resA)
        ins = []
        for c in range(8):
            d = {"xT": xT[c], "cmask": per_rank[c % 4][0], "force": per_rank[c % 4][1]}
            d.update(lw)
            d.update(ex[c])
            d.update(consts)
            ins.append(d)
        resB = run_bass_kernel_spmd(ncB, ins, core_ids=list(range(8))).results
        xT = [np.asarray(resB[c]["xT_out"], np.float32) for c in range(8)]
    return unshard(xT)
```

```python
import contextlib
import numpy as np
import ml_dtypes
import concourse.bass as bass
import concourse.mybir as mybir
from concourse.bass_utils import run_bass_kernel_spmd

F32 = mybir.dt.float32
BF16 = mybir.dt.bfloat16
AF = mybir.ActivationFunctionType
ALU = mybir.AluOpType

DEPTH = 4
ALPHA = (2 * DEPTH) ** 0.25
LN_EPS = 1e-5
BIG = 30000.0
GS = [[r, 7 - r, 8 + r, 15 - r] for r in range(4)]
NSLOT = [16, 32, 48, 64]
SLOT_OFF = [0, 16, 48, 96]
PASS_TILES = [(0, 3), (1, 2)]
ENGS = ("pe", "act", "dve", "pool", "sp")


class Prog:
    def __init__(self, nc):
        self.nc = nc
        self.ops = []
        self.st = {}
        self.streams = {}

    def _recs(self, key):
        if isinstance(key, str):
            key = (key,)
        ent = self.st.setdefault(key[0], {"whole": {"w": None, "r": []}, "sub": {}})
        if len(key) == 1:
            return ent, None
        idx = key[1:]
        if idx not in ent["sub"]:
            ent["sub"][idx] = {"w": ent["whole"]["w"], "r": list(ent["whole"]["r"])}
        return ent, idx

    def op(self, eng, fn, reads=(), writes=(), dma=None, ninc=1):
        i = len(self.ops)
        deps = {}

        def add(j, kind):
            if j is None:
                return
            if j in deps and deps[j] != "war":
                return
            deps[j] = kind

        for key in reads:
            ent, idx = self._recs(key)
            if idx is None:
                add(ent["whole"]["w"], "raw")
                for s in ent["sub"].values():
                    add(s["w"], "raw")
            else:
                add(ent["sub"][idx]["w"], "raw")
                if key[0] == "ps":
                    for r in ent["sub"][idx]["r"]:
                        if self.ops[r]["eng"] != eng:
                            add(r, "rar")
        for key in writes:
            ent, idx = self._recs(key)
            if idx is None:
                add(ent["whole"]["w"], "waw")
                for r in ent["whole"]["r"]:
                    add(r, "war")
                for s in ent["sub"].values():
                    add(s["w"], "waw")
                    for r in s["r"]:
                        add(r, "war")
            else:
                s = ent["sub"][idx]
                add(s["w"], "waw")
                for r in s["r"]:
                    add(r, "war")
        for key in reads:
            ent, idx = self._recs(key)
            if idx is None:
                ent["whole"]["r"].append(i)
                for s in ent["sub"].values():
                    s["r"].append(i)
            else:
                ent["sub"][idx]["r"].append(i)
        for key in writes:
            ent, idx = self._recs(key)
            if idx is None:
                ent["whole"] = {"w": i, "r": []}
                ent["sub"] = {}
            else:
                ent["sub"][idx] = {"w": i, "r": []}
        deps.pop(i, None)
        if dma is not None:
            self.streams.setdefault(dma, 0)
        self.ops.append({"eng": eng, "fn": fn, "deps": deps, "dma": dma, "ninc": ninc,
                         "sig": False, "val": None})
        return i

    def emit(self, final_wait_engine="sp"):
        nc = self.nc
        ops = self.ops
        need = []
        for i, o in enumerate(ops):
            w = []
            for j, kind in o["deps"].items():
                p = ops[j]
                if p["dma"] is None and p["eng"] == o["eng"] and o["dma"] is None:
                    if o["eng"] == "pe":
                        continue
                w.append(j)
                p["sig"] = True
            need.append(w)
        cnt = {e: 0 for e in ENGS}
        scnt = {s: 0 for s in self.streams}
        for o in ops:
            if o["dma"] is not None:
                scnt[o["dma"]] += 16 * o["ninc"]
                o["val"] = scnt[o["dma"]]
            elif o["sig"]:
                cnt[o["eng"]] += 1
                o["val"] = cnt[o["eng"]]
        with contextlib.ExitStack() as es:
            esem = {e: es.enter_context(nc.semaphore("c_" + e)) for e in ENGS}
            ssem = {s: es.enter_context(nc.semaphore("d_" + s)) for s in self.streams}
            block = es.enter_context(nc.Block())
            bmap = {"pe": block.tensor, "act": block.scalar, "dve": block.vector,
                    "pool": block.gpsimd, "sp": block.sync}
            for e in ENGS:
                mine = [(i, o) for i, o in enumerate(ops) if o["eng"] == e]
                if not mine and e != final_wait_engine:
                    continue

                def body(engobj, mine=mine, e=e):
                    waited = {}
                    for i, o in mine:
                        for j in need[i]:
                            p = ops[j]
                            if p["dma"] is not None:
                                k, sem = ("s", p["dma"]), ssem[p["dma"]]
                            else:
                                k, sem = ("e", p["eng"]), esem[p["eng"]]
                            if waited.get(k, 0) >= p["val"]:
                                continue
                            engobj.wait_ge(sem, p["val"])
                            waited[k] = p["val"]
                        r = o["fn"](engobj)
                        if o["dma"] is not None:
                            rl = r if isinstance(r, (list, tuple)) else [r]
                            assert len(rl) == o["ninc"]
                            for ins in rl:
                                ins.then_inc(ssem[o["dma"]], 16)
                        elif o["sig"]:
                            r.then_inc(esem[e], 1)
                    if e == final_wait_engine:
                        for s, v in scnt.items():
                            if v > 0:
                                engobj.wait_ge(ssem[s], v)

                bmap[e](body)


def MM(P, out, lhsT, rhs, start, stop, reads, writes):
    P.op("pe", lambda e: e.matmul(out, lhsT=lhsT, rhs=rhs, start=start, stop=stop), reads, writes)


def TR(P, out, in_, ident, reads, writes):
    P.op("pe", lambda e: e.transpose(out, in_, ident), reads, writes)


def ACT(P, out, in_, func, reads, writes, bias=0.0, scale=1.0):
    P.op("act", lambda e: e.activation(out=out, in_=in_, func=func, bias=bias, scale=scale), reads, writes)


def TT(P, eng, out, in0, in1, op, reads, writes):
    P.op(eng, lambda e: e.tensor_tensor(out=out, in0=in0, in1=in1, op=op), reads, writes)


def TS(P, eng, out, in0, s1, s2, op0, op1, reads, writes):
    if op1 is None:
        P.op(eng, lambda e: e.tensor_scalar(out=out, in0=in0, scalar1=s1, scalar2=None, op0=op0), reads, writes)
    else:
        P.op(eng, lambda e: e.tensor_scalar(out=out, in0=in0, scalar1=s1, scalar2=s2, op0=op0, op1=op1),
             reads, writes)


def STT(P, eng, out, in0, scalar, in1, op0, op1, reads, writes):
    P.op(eng, lambda e: e.scalar_tensor_tensor(out=out, in0=in0, scalar=scalar, in1=in1, op0=op0, op1=op1),
         reads, writes)


def CP(P, eng, out, in_, reads, writes):
    P.op(eng, lambda e: e.tensor_copy(out=out, in_=in_), reads, writes)


def OP1(P, eng, name, out, in_, reads, writes):
    P.op(eng, lambda e: getattr(e, name)(out=out, in_=in_), reads, writes)


def DMA(P, eng, out, in_, reads, writes, stream):
    P.op(eng, lambda e: e.dma_start(out=out, in_=in_), reads, writes, dma=stream)


def DMAS(P, eng, pairs, reads, writes, stream):
    P.op(eng, lambda e: [e.dma_start(out=o, in_=i) for (o, i) in pairs], reads, writes, dma=stream, ninc=len(pairs))


def LOADC(P, stage, dst, src, dims, dkeys, np_=128):
    k0 = dkeys[0]
    stream = k0 if isinstance(k0, str) else "_".join(str(t) for t in k0)
    DMA(P, "pool", dst, src, [], dkeys, "ld_" + stream)


def FENCE(P, dummy, old, new):
    P.op("dve", lambda e: e.memset(dummy, 0.0), reads=[], writes=list(old) + list(new) + ["fence_dummy"])


def build_A():
    nc = bass.Bass("TRN2", target_bir_lowering=False)
    Din = lambda n, s, dt=F32: nc.dram_tensor(n, s, dt, kind="ExternalInput").ap()
    Dout = lambda n, s, dt=F32: nc.dram_tensor(n, s, dt, kind="ExternalOutput").ap()
    xT = Din("xT", [8, 128, 2048])
    w_in_t = Din("w_in_t", [16, 128, 8, 512])
    K_loc = Dout("K_loc", [4, 128, 2048], BF16)
    V_loc = Dout("V_loc", [4, 128, 16, 2, 80], BF16)
    km_loc = Dout("km_loc", [128, 4, 8])
    tail_loc = Dout("tail_loc", [128, 4, 4, 30])
    P = Prog(nc)
    with contextlib.ExitStack() as es:
        T = lambda n, s, dt: es.enter_context(nc.sbuf_tensor(n, s, dt))
        xb = T("xb", [128, 8, 2048], BF16)
        wseg = [T("wseg%d" % i, [128, 8, 512], BF16) for i in range(2)]
        Kt = T("Kt", [128, 4, 2048], BF16)
        Vt = T("Vt", [128, 16, 8, 80], BF16)
        km = T("km", [128, 4, 8], F32)
        ks = T("ks", [128, 2], F32)
        bv = T("bv", [128, 4, 4, 30], F32)
        sg = T("sg", [128, 30], F32)
        tl = T("tl", [128, 4, 4, 30], F32)
        ps = [es.enter_context(nc.psum_tensor("ps%d" % i, [128, 512], F32)) for i in range(4)]
        stage = None
        for k2 in range(4):
            LOADC(P, stage, xb[:, 2 * k2:2 * k2 + 2, :], xT[2 * k2:2 * k2 + 2].rearrange("k p c -> p k c"),
                  (2, 2048), [("xb", k2)])
        import os
        SK = os.environ.get("A_SKIP", "")
        if "m" not in SK:
            P.op("dve", lambda e: e.memset(Vt[:, :, :, 64:80], 0.0), [], ["Vt"])
            P.op("dve", lambda e: e.memset(Vt[:, :, :, 64:65], 1.0), ["Vt"], ["Vt"])
        segs = [7, 8, 3, 4]
        bank = 0

        def load(si):
            wk = "wseg%d" % (si % 2)
            LOADC(P, stage, wseg[si % 2][:], w_in_t[segs[si]], (8, 512), [wk])

        load(0)
        load(1)
        w, wk = wseg[0], "wseg0"
        for hp in range(4):
            for t in range(4):
                pb = ps[bank % 4]; pk = ("ps", bank % 4); bank += 1
                for kc in range(8):
                    MM(P, pb[:], w[:, kc, hp * 128:(hp + 1) * 128], xb[:, kc, t * 512:(t + 1) * 512],
                       kc == 0, kc == 7, [wk, "xb"], [pk])
                ACT(P, Kt[:, hp, t * 512:(t + 1) * 512], pb[:], AF.Identity, [pk], [("Kt", hp)])
                if "r" not in SK:
                    P.op("dve", lambda e, pb=pb: e.tensor_reduce(
                        out=ks[:], in_=pb[:].rearrange("p (b k) -> p b k", b=2), axis=mybir.AxisListType.X,
                        op=ALU.add), [pk], ["ks"])
                    TS(P, "dve", km[:, hp, 2 * t:2 * t + 2], ks[:], 1.0 / 256, None, ALU.mult, None, ["ks"], ["km"])
            if "o" not in SK:
                DMA(P, "sp", K_loc[hp], Kt[:, hp, :], [("Kt", hp)], ["K_loc%d" % hp], "Kout")
        if "r" not in SK and "d" not in SK:
            DMA(P, "sp", km_loc, km[:], ["km"], ["km_loc"], "kmout")
        import os
        if os.environ.get("A_STOP") == "k":
            P.emit()
            return nc
        load(2)
        w, wk = wseg[1], "wseg1"
        for ch in range(16):
            pb = ps[bank % 4]; pk = ("ps", bank % 4); bank += 1
            for kc in range(8):
                MM(P, pb[:], xb[:, kc, ch * 128:(ch + 1) * 128], w[:, kc, :], kc == 0, kc == 7,
                   [wk, "xb"], [pk])
            if ch % 2:
                CP(P, "dve", Vt[:, ch, :, 0:64], pb[:].rearrange("p (h d) -> p h d", h=8), [pk], [("Vt", ch)])
            else:
                ACT(P, Vt[:, ch, :, 0:64], pb[:].rearrange("p (h d) -> p h d", h=8), AF.Identity, [pk], [("Vt", ch)])
        for hp in range(4):
            DMA(P, "sp", V_loc[hp], Vt[:, :, 2 * hp:2 * hp + 2, :], ["Vt"], ["V_loc%d" % hp], "Vout")
        if os.environ.get("A_STOP") == "v":
            P.emit()
            return nc
        load(3)
        for si, dst in ((2, "val"), (3, "glu")):
            w, wk = wseg[si % 2], "wseg%d" % (si % 2)
            for cb in range(4):
                for t in range(4):
                    pb = ps[bank % 4]; pk = ("ps", bank % 4); bank += 1
                    c0 = t * 512 + 384
                    for kc in range(8):
                        MM(P, pb[:, 0:128], w[:, kc, cb * 128:(cb + 1) * 128], xb[:, kc, c0:c0 + 128],
                           kc == 0, kc == 7, [wk, "xb"], [pk])
                    if dst == "val":
                        CP(P, "dve", bv[:, cb, t, :], pb[:, 98:128], [pk], [("bv", cb, t)])
                    else:
                        ACT(P, sg[:], pb[:, 98:128], AF.Sigmoid, [pk], ["sg"])
                        TT(P, "dve", tl[:, cb, t, :], bv[:, cb, t, :], sg[:], ALU.mult,
                           [("bv", cb, t), "sg"], [("tl", cb, t)])
        DMA(P, "sp", tail_loc, tl[:], ["tl"], ["tail_loc"], "tlout")
        P.emit()
    return nc


def build_B():
    nc = bass.Bass("TRN2", target_bir_lowering=False)
    Din = lambda n, s, dt=F32: nc.dram_tensor(n, s, dt, kind="ExternalInput").ap()
    xT = Din("xT", [8, 128, 2048])
    w_in_t = Din("w_in_t", [16, 128, 8, 512])
    wm_t = Din("wm_t", [8, 128, 3, 8, 128])
    wb_t = Din("wb_t", [8, 128, 2, 4, 128])
    wb2_t = Din("wb2_t", [8, 64, 8, 128])
    wo_t = Din("wo_t", [128, 8, 1024])
    sgwT = Din("sgwT", [128, 4, 128])
    sgb_bc = Din("sgb_bc", [128, 4, 128])
    vg_bc = Din("vg_bc", [128, 512])
    vb_bc = Din("vb_bc", [128, 512])
    convwT = Din("convwT", [128, 4, 31])
    colp = Din("colp", [128, 3, 4])
    lnp = Din("lnp", [128, 2, 8])
    K_arr = Din("K_arr", [8, 64, 160 * 128], BF16)
    V_arr = Din("V_arr", [8, 128, 160, 80], BF16)
    km_arr = Din("km_arr", [64, 8, 4, 32])
    eauxk_d = Din("eauxk", [32, 64 * 128], BF16)
    halo = Din("halo", [128, 4, 4, 30])
    identF_d = Din("identF", [128, 128])
    maskT_d = Din("maskT", [128, 128])
    alibi_d = Din("alibi", [128, 8, 64])
    cmask_d = Din("cmask", [128, 4, 2, 32])
    force_d = Din("force", [128, 4, 2, 32])
    shift_d = Din("shiftc", [128, 4, 8])
    ones_d = Din("onesF", [128, 128])
    xT_out = nc.dram_tensor("xT_out", [8, 128, 2048], F32, kind="ExternalOutput").ap()

    P = Prog(nc)
    import os
    STOP = os.environ.get("B_STOP", "")
    with contextlib.ExitStack() as es:
        T = lambda n, s, dt: es.enter_context(nc.sbuf_tensor("s_" + n, s, dt))
        identF = T("identF", [128, 128], F32)
        onesF = T("onesF", [128, 128], F32)
        maskT = T("maskT", [128, 128], F32)
        tri_bf = T("tri_bf", [128, 128], BF16)
        alibi = T("alibi", [128, 8, 64], F32)
        cmask = T("cmask", [128, 4, 2, 32], F32)
        force = T("force", [128, 4, 2, 32], F32)
        shiftc = T("shiftc", [128, 4, 8], F32)
        sgwT_s = T("sgwT_s", [128, 4, 128], F32)
        WmT = T("WmT", [128, 4, 128], BF16)
        sgb = T("sgb", [128, 4, 128], F32)
        vg = T("vg", [128, 512], F32)
        vb = T("vb", [128, 512], F32)
        convw = T("convw", [128, 4, 31], F32)
        colp_s = T("colp_s", [128, 3, 4], F32)
        lnp_s = T("lnp_s", [128, 2, 8], F32)
        kmb = T("kmb", [64, 8, 4, 32], BF16)
        dummy = T("dummy", [128, 2], F32)
        epsc = T("epsc", [128, 1], F32)
        P.op("dve", lambda e: e.memset(epsc[:], LN_EPS), [], ["epsc"])
        stage = T("stage", [128, 4096], F32)
        ps = [es.enter_context(nc.psum_tensor("ps%d" % i, [128, 512], F32)) for i in range(8)]
        PK = lambda i: ("ps", i)

        for (dst, src, name) in ((identF, identF_d, "identF"), (onesF, ones_d, "onesF"), (maskT, maskT_d, "maskT"),
                                 (alibi, alibi_d, "alibi"), (cmask, cmask_d, "cmask"), (force, force_d, "force"),
                                 (shiftc, shift_d, "shiftc"), (sgwT_s, sgwT, "sgwT_s"), (sgb, sgb_bc, "sgb"),
                                 (vg, vg_bc, "vg"), (vb, vb_bc, "vb"), (convw, convwT, "convw"),
                                 (colp_s, colp, "colp_s"), (lnp_s, lnp, "lnp_s")):
            DMA(P, "sp", dst[:], src, [], [name], "c_" + name)
        DMA(P, "pool", kmb[:], km_arr, [], ["kmb"], "c_kmb")
        CP(P, "dve", tri_bf[:], maskT[:], ["maskT"], ["tri_bf"])
        for g in range(4):
            TT(P, "dve", WmT[:, g, :], sgwT_s[:, g, :], maskT[:], ALU.mult, ["sgwT_s", "maskT"], ["WmT"])

        for pidx, tl in enumerate(PASS_TILES):
            sfx = "_p%d" % pidx
            with contextlib.ExitStack() as pes:
                TP = lambda n, s, dt: pes.enter_context(nc.sbuf_tensor("s_" + n + sfx, s, dt))
                xb = TP("xb", [128, 8, 2, 512], BF16)
                wseg = [TP("wseg%d" % i, [128, 8, 512], BF16) for i in range(2)]
                y_a = TP("y_a", [128, 4, 2, 512], BF16)
                y_b = TP("y_b", [128, 4, 2, 512], BF16)
                y_c = TP("y_c", [64, 8, 2, 512], BF16)
                tmp = [TP("tmp%d" % i, [128, 512], F32) for i in range(2)]
                N = lambda s: s + sfx
                glob_new = [N("xb"), N("wseg0"), N("wseg1"), N("y_a"), N("y_b"), N("y_c"), N("tmp0"), N("tmp1")]
                wcount = [0]

                def load_seg(seg):
                    i = wcount[0] % 2
                    wcount[0] += 1
                    LOADC(P, stage, wseg[i][:], w_in_t[seg], (8, 512), [N("wseg%d" % i)])
                    return wseg[i], N("wseg%d" % i)

                def load_xb():
                    for tt in range(2):
                        LOADC(P, stage, xb[:, :, tt, :],
                              xT[:, :, tl[tt] * 512:(tl[tt] + 1) * 512].rearrange("k p c -> p k c"), (8, 512),
                              [(N("xb"), tt)])
                bankc = [0]

                def nb():
                    b = bankc[0] % 2
                    bankc[0] += 1
                    return b

                tmpc = [0]

                def nt():
                    i = tmpc[0] % 2
                    tmpc[0] += 1
                    return tmp[i], N("tmp%d" % i)

                def proj_fm(w, wk, cb, tt, M=128, c0=None):
                    b = nb()
                    c0 = cb * 128 if c0 is None else c0
                    for kc in range(8):
                        MM(P, ps[b][0:M, :], w[:, kc, c0:c0 + M], xb[:, kc, tt, :], kc == 0, kc == 7,
                           [wk, (N("xb"), tt)], [PK(b)])
                    return b

                with contextlib.ExitStack() as ph:
                    T1 = lambda n, s, dt: ph.enter_context(nc.sbuf_tensor("s_" + n + sfx, s, dt))
                    zt = T1("zt", [128, 4, 2, 512], F32)
                    vLN = T1("vLN", [128, 8, 512], BF16)
                    glu = T1("glu", [128, 4, 2, 542], BF16)
                    diag = [T1("diag%d" % i, [128, 31, 128], BF16) for i in range(2)]
                    cvraw = T1("cvraw", [128, 4, 2, 512], F32)
                    st6 = T1("st6", [128, 6], F32)
                    mv = T1("mv", [128, 2], F32)
                    rstd = T1("rstd", [128, 1], F32)
                    mS = T1("mS", [128, 512], F32)
                    rS = T1("rS", [128, 512], F32)
                    new1 = [N(s_) for s_ in ("zt", "vLN", "glu", "diag0", "diag1", "cvraw", "st6", "mv", "rstd",
                                             "mS", "rS")]
                    FENCE(P, dummy[:, 0:1], [], glob_new + new1)
                    load_xb()
                    for tt in range(2):
                        LOADC(P, stage, glu[:, :, tt, 0:30], halo[:, :, tl[tt], :], (4, 30),
                              [(N("glu"), cb, tt, "h") for cb in range(4)])
                    w, wk = load_seg(1)
                    wu, wuk = load_seg(0)
                    for tt in range(2):
                        for ci in range(4):
                            b = nb()
                            ch = tt * 4 + ci
                            for kc in range(8):
                                MM(P, ps[b][:], xb[:, kc, tt, ci * 128:(ci + 1) * 128], w[:, kc, :], kc == 0, kc == 7,
                                   [wk, (N("xb"), tt)], [PK(b)])
                            OP1(P, "dve", "bn_stats", st6[:], ps[b][:], [PK(b)], [N("st6")])
                            OP1(P, "dve", "bn_aggr", mv[:], st6[:], [N("st6")], [N("mv")])
                            ACT(P, rstd[:], mv[:, 1:2], AF.Sqrt, [N("mv"), "epsc"], [N("rstd")], bias=epsc[:])
                            OP1(P, "dve", "reciprocal", rstd[:], rstd[:], [N("rstd")], [N("rstd")])
                            t_, tk = nt()
                            TS(P, "dve", t_[:], ps[b][:], mv[:, 0:1], rstd[:], ALU.subtract, ALU.mult,
                               [PK(b), N("mv"), N("rstd")], [tk])
                            TT(P, "dve", t_[:], t_[:], vg[:], ALU.mult, [tk, "vg"], [tk])
                            TT(P, "dve", vLN[:, ch, :], t_[:], vb[:], ALU.add, [tk, "vb"], [(N("vLN"), ch)])
                    if STOP == "av":
                        P.emit()
                        return nc
                    for tt in range(2):
                        for g in range(4):
                            b = 2 + (g % 2)
                            for ci in range(4):
                                ch = tt * 4 + ci
                                MM(P, ps[b][:, ci * 128:(ci + 1) * 128], vLN[:, ch, g * 128:(g + 1) * 128],
                                   WmT[:, g, :], True, True, [(N("vLN"), ch), "WmT"], [PK(b)])
                            for ci in range(4):
                                TT(P, "dve", zt[:, g, tt, ci * 128:(ci + 1) * 128], ps[b][:, ci * 128:(ci + 1) * 128],
                                   sgb[:, g, :], ALU.add, [PK(b), "sgb"], [(N("zt"), g, tt)])
                    if STOP == "sp":
                        P.emit()
                        return nc
                    w, wk = wu, wuk
                    wg, wgk = load_seg(2)
                    for cb in range(4):
                        for tt in range(2):
                            b = proj_fm(w, wk, cb, tt)
                            TT(P, "dve", zt[:, cb, tt, :], ps[b][:], zt[:, cb, tt, :], ALU.mult,
                               [PK(b), (N("zt"), cb, tt)], [(N("zt"), cb, tt)])
                    w, wk = wg, wgk
                    wv, wvk = load_seg(3)
                    for cb in range(4):
                        for tt in range(2):
                            b = proj_fm(w, wk, cb, tt)
                            t_, tk = nt()
                            ACT(P, t_[:], ps[b][:], AF.Silu, [PK(b)], [tk])
                            TT(P, "dve", y_a[:, cb, tt, :], zt[:, cb, tt, :], t_[:], ALU.mult,
                               [tk, (N("zt"), cb, tt)], [(N("y_a"), cb, tt)])
                    if STOP == "ya":
                        P.emit()
                        return nc
                    w, wk = wv, wvk
                    wl, wlk = load_seg(4)
                    for cb in range(4):
                        for tt in range(2):
                            b = proj_fm(w, wk, cb, tt)
                            CP(P, "dve", zt[:, cb, tt, :], ps[b][:], [PK(b)], [(N("zt"), cb, tt)])
                    w, wk = wl, wlk
                    wbg, wbgk = load_seg(5)
                    for cb in range(4):
                        for tt in range(2):
                            b = proj_fm(w, wk, cb, tt)
                            t_, tk = nt()
                            ACT(P, t_[:], ps[b][:], AF.Sigmoid, [PK(b)], [tk])
                            TT(P, "dve", glu[:, cb, tt, 30:542], zt[:, cb, tt, :], t_[:], ALU.mult,
                               [tk, (N("zt"), cb, tt)], [(N("glu"), cb, tt, "m")])
                    if STOP == "glu":
                        P.emit()
                        return nc
                    for cb in range(4):
                        dg, dgk = diag[cb % 2], N("diag%d" % (cb % 2))
                        for j in range(31):
                            TS(P, "dve", dg[:, j, :], identF[:], convw[:, cb, j:j + 1], None, ALU.mult, None,
                               ["identF", "convw"], [dgk])
                        for tt in range(2):
                            b = 2 + (tt % 2)
                            for j in range(31):
                                MM(P, ps[b][:], dg[:, j, :], glu[:, cb, tt, j:j + 512], j == 0, j == 30,
                                   [dgk, (N("glu"), cb, tt, "h"), (N("glu"), cb, tt, "m")], [PK(b)])
                            ACT(P, cvraw[:, cb, tt, :], ps[b][:], AF.Identity, [PK(b), "colp_s"],
                                [(N("cvraw"), cb, tt)], bias=colp_s[:, 0, cb:cb + 1])
                    if STOP == "conv":
                        P.emit()
                        return nc
                    for tt in range(2):
                        for cb in range(4):
                            t_, tk = nt()
                            ACT(P, t_[:], cvraw[:, cb, tt, :], AF.Square, [(N("cvraw"), cb, tt)], [tk])
                            MM(P, ps[4][:], onesF[:], cvraw[:, cb, tt, :], cb == 0, cb == 3,
                               ["onesF", (N("cvraw"), cb, tt)], [PK(4)])
                            MM(P, ps[5][:], onesF[:], t_[:], cb == 0, cb == 3, ["onesF", tk], [PK(5)])
                        TS(P, "dve", mS[:], ps[4][:], 1.0 / 512, None, ALU.mult, None, [PK(4)], [N("mS")])
                        t_, tk = nt()
                        TT(P, "dve", t_[:], mS[:], mS[:], ALU.mult, [N("mS")], [tk])
                        STT(P, "dve", rS[:], ps[5][:], 1.0 / 512, t_[:], ALU.mult, ALU.subtract, [PK(5), tk], [N("rS")])
                        ACT(P, rS[:], rS[:], AF.Sqrt, [N("rS"), "epsc"], [N("rS")], bias=epsc[:])
                        OP1(P, "dve", "reciprocal", rS[:], rS[:], [N("rS")], [N("rS")])
                        for cb in range(4):
                            t_, tk = nt()
                            TT(P, "dve", t_[:], cvraw[:, cb, tt, :], mS[:], ALU.subtract,
                               [(N("cvraw"), cb, tt), N("mS")], [tk])
                            TT(P, "dve", t_[:], t_[:], rS[:], ALU.mult, [tk, N("rS")], [tk])
                            ACT(P, zt[:, cb, tt, :], t_[:], AF.Silu, [tk, "colp_s"], [(N("zt"), cb, tt)],
                                bias=colp_s[:, 2, cb:cb + 1], scale=colp_s[:, 1, cb:cb + 1])
                    if STOP == "cvln":
                        P.emit()
                        return nc
                    w, wk = wbg, wbgk
                    wq, wqk = load_seg(6)
                    for cb in range(4):
                        for tt in range(2):
                            b = proj_fm(w, wk, cb, tt)
                            t_, tk = nt()
                            ACT(P, t_[:], ps[b][:], AF.Silu, [PK(b)], [tk])
                            TT(P, "dve", y_b[:, cb, tt, :], zt[:, cb, tt, :], t_[:], ALU.mult,
                               [tk, (N("zt"), cb, tt)], [(N("y_b"), cb, tt)])
                    old1 = new1

                if STOP == "p1":
                    P.emit()
                    return nc
                with contextlib.ExitStack() as ph:
                    T2 = lambda n, s, dt: ph.enter_context(nc.sbuf_tensor("s_" + n + sfx, s, dt))
                    Qcat = T2("Qcat", [96, 8, 2, 512], BF16)
                    pen = T2("pen", [128, 8, 8, 32], F32)
                    gsb = T2("gsb", [128, 8, 32], F32)
                    top8 = T2("top8", [128, 8, 8], F32)
                    qtmp = [T2("qtmp%d" % i, [32, 2, 512], BF16) for i in range(2)]
                    Kb = [T2("Kb%d" % i, [96, NSLOT[tl[i]] * 128], BF16) for i in range(2)]
                    Vb = [T2("Vb%d" % i, [128, NSLOT[tl[i]], 80], BF16) for i in range(2)]
                    PT = [T2("PT%d" % i, [128, 512], BF16) for i in range(4)]
                    wcg = T2("wcg", [128, 8, 512], BF16)
                    sgc = T2("sgc", [64, 512], F32)
                    rec = T2("rec", [65, 512], F32)
                    tat = T2("tat", [64, 512], F32)
                    new2 = [N(s) for s in ("Qcat", "pen", "gsb", "top8", "qtmp0", "qtmp1", "Kb0", "Kb1", "Vb0", "Vb1",
                                           "PT0", "PT1", "PT2", "PT3", "wcg", "sgc", "rec", "tat")]
                    FENCE(P, dummy[:, 0:1], old1, new2)
                    DMA(P, "pool", wcg[:], w_in_t[9], [], [N("wcg")], N("wcg"))
                    for i in range(2):
                        n_ = NSLOT[tl[i]]
                        DMA(P, "sp", Kb[i][64:96, :], eauxk_d[:, 0:n_ * 128], [], [(N("Kb%d" % i), "e")], N("Kbe%d" % i))
                    w, wk = wq, wqk
                    for h in range(8):
                        for tt in range(2):
                            b = proj_fm(w, wk, 0, tt, M=64, c0=h * 64)
                            TS(P, "dve", Qcat[0:64, h, tt, :], ps[b][0:64, :], 0.125, None, ALU.mult, None, [PK(b)],
                               [(N("Qcat"), h, tt, "q")])
                    if STOP == "q":
                        P.emit()
                        return nc
                    for tt in range(2):
                        ti = tl[tt]
                        for ci in range(4):
                            ch = tt * 4 + ci
                            hf = ci // 2
                            for h in range(8):
                                MM(P, ps[2][:, h * 32:(h + 1) * 32], Qcat[0:64, h, tt, ci * 128:(ci + 1) * 128],
                                   kmb[:, h, ti, :], True, True, [(N("Qcat"), h, tt, "q"), "kmb"], [PK(2)])
                            for h in range(8):
                                TT(P, "dve", gsb[:, h, :], ps[2][:, h * 32:(h + 1) * 32], cmask[:, ti, hf, :],
                                   ALU.add, [PK(2), "cmask"], [(N("gsb"), h)])
                                OP1(P, "dve", "max", top8[:, h, :], gsb[:, h, :], [(N("gsb"), h)], [(N("top8"), h)])
                                TS(P, "dve", gsb[:, h, :], gsb[:, h, :], top8[:, h, 2:3], -BIG, ALU.is_lt, ALU.mult,
                                   [(N("gsb"), h), (N("top8"), h)], [(N("gsb"), h)])
                                STT(P, "dve", pen[:, ch, h, :], gsb[:, h, :], shiftc[:, ci, h:h + 1],
                                    force[:, ti, hf, :], ALU.add, ALU.add, [(N("gsb"), h), "shiftc", "force"],
                                    [(N("pen"), ch, h)])
                    if STOP == "gate":
                        P.emit()
                        return nc
                    for h in range(8):
                        qt_, qtk = qtmp[h % 2], N("qtmp%d" % (h % 2))
                        for tt in range(2):
                            bT = 2 + (tt % 2)
                            for ci in range(4):
                                ch = tt * 4 + ci
                                TR(P, ps[bT][0:32, ci * 128:(ci + 1) * 128], pen[:, ch, h, :], identF[:],
                                   [(N("pen"), ch, h), "identF"], [PK(bT)])
                            CP(P, "dve", qt_[:, tt, :], ps[bT][0:32, :], [PK(bT)], [(qtk, tt)])
                        DMA(P, "sp", Qcat[64:96, h, :, :], qt_[:], [qtk], [(N("Qcat"), h, 0, "a"), (N("Qcat"), h, 1, "a")],
                            N("qauxd%d" % (h % 2)))
                    kvc = [0]
                    pending = {}

                    def load_kv(h, tt):
                        i = kvc[0] % 2
                        kvc[0] += 1
                        ti = tl[tt]
                        n = NSLOT[ti]
                        o = SLOT_OFF[ti]
                        DMA(P, "sp", Kb[i][0:64, 0:n * 128], K_arr[h, :, o * 128:(o + n) * 128], [],
                            [(N("Kb%d" % i), "k")], N("Kb%d" % i))
                        DMA(P, "sp", Vb[i][:, 0:n, :], V_arr[h, :, o:o + n, :], [], [N("Vb%d" % i)], N("Vb%d" % i))
                        pending[(h, tt)] = i

                    seq = [(h, tt) for h in range(8) for tt in range(2)]
                    load_kv(*seq[0])
                    sb = [0]
                    pc = [0]
                    accc = [0]
                    for si, (h, tt) in enumerate(seq):
                        if si + 1 < len(seq):
                            load_kv(*seq[si + 1])
                        i = pending[(h, tt)]
                        kb_, kk = Kb[i], N("Kb%d" % i)
                        vb_, vk = Vb[i], N("Vb%d" % i)
                        ti = tl[tt]
                        n = NSLOT[ti]
                        acc = 6 + (accc[0] % 2)
                        accc[0] += 1
                        units = list(range(n - 1, -1, -1))
                        staged = []

                        def qk_unit(rel):
                            c = 3 - rel if rel <= 3 else 0
                            c0 = 128 * c
                            sbank = 3 + (sb[0] % 3)
                            sb[0] += 1
                            MM(P, ps[sbank][:, c0:512], kb_[0:96, rel * 128:(rel + 1) * 128], Qcat[0:96, h, tt, c0:512],
                               True, True, [(kk, "k"), (kk, "e"), (N("Qcat"), h, tt, "q"), (N("Qcat"), h, tt, "a")],
                               [PK(sbank)])
                            pi = pc[0] % 4
                            pc[0] += 1
                            ACT(P, PT[pi][:, c0:512], ps[sbank][:, c0:512], AF.Exp, [PK(sbank), "alibi"],
                                [N("PT%d" % pi)], bias=alibi[:, h, rel:rel + 1])
                            if rel <= 3:
                                TT(P, "dve", PT[pi][:, c0:c0 + 128], PT[pi][:, c0:c0 + 128], tri_bf[:], ALU.mult,
                                   [N("PT%d" % pi), "tri_bf"], [N("PT%d" % pi)])
                            return (rel, pi, c0)

                        def pv_unit(u, first, last):
                            rel, pi, c0 = u
                            MM(P, ps[acc][0:65, c0:512], vb_[:, rel, 0:65], PT[pi][:, c0:512], first, last,
                               [vk, N("PT%d" % pi)], [PK(acc)])

                        done = 0
                        for ui, rel in enumerate(units):
                            staged.append(qk_unit(rel))
                            if len(staged) > 2:
                                pv_unit(staged.pop(0), done == 0, False)
                                done += 1
                        while staged:
                            u = staged.pop(0)
                            pv_unit(u, done == 0, len(staged) == 0)
                            done += 1
                        for kc in range(8):
                            MM(P, ps[0][0:64, :], wcg[:, kc, h * 64:(h + 1) * 64], xb[:, kc, tt, :], kc == 0, kc == 7,
                               [N("wcg"), (N("xb"), tt)], [PK(0)])
                        ACT(P, sgc[:], ps[0][0:64, :], AF.Silu, [PK(0)], [N("sgc")])
                        OP1(P, "dve", "reciprocal", rec[64:65, :], ps[acc][64:65, :], [PK(acc)], [N("rec")])
                        MM(P, ps[1][0:64, :], onesF[64:65, 0:64], rec[64:65, :], True, True, ["onesF", N("rec")],
                           [PK(1)])
                        TT(P, "dve", tat[:], ps[acc][0:64, :], sgc[:], ALU.mult, [PK(acc), N("sgc")], [N("tat")])
                        TT(P, "dve", y_c[:, h, tt, :], tat[:], ps[1][0:64, :], ALU.mult, [N("tat"), PK(1)],
                           [(N("y_c"), h, tt)])
                    old2 = new2

                if STOP == "p2":
                    P.emit()
                    return nc
                with contextlib.ExitStack() as ph:
                    T3 = lambda n, s, dt: ph.enter_context(nc.sbuf_tensor("s_" + n + sfx, s, dt))
                    mixb = T3("mixb", [128, 8, 2, 512], BF16)
                    wm = [T3("wm%d" % i, [128, 3, 8, 128], BF16) for i in range(2)]
                    wbr = [T3("wbr%d" % i, [128, 2, 4, 128], BF16) for i in range(2)]
                    wb2 = [T3("wb2%d" % i, [64, 8, 128], BF16) for i in range(2)]
                    wo = T3("wo", [128, 8, 1024], BF16)
                    xres = T3("xres", [128, 8, 512], F32)
                    rr = T3("rr", [128, 8, 512], F32)
                    mix = T3("mix", [128, 512], F32)
                    t2 = T3("t2", [128, 512], F32)
                    mS = T3("mS3", [128, 512], F32)
                    rS = T3("rS3", [128, 512], F32)
                    new3 = [N(s) for s in ("mixb", "wm0", "wm1", "wbr0", "wbr1", "wb20", "wb21", "wo", "xres", "rr",
                                           "mix", "t2", "mS3", "rS3")]
                    FENCE(P, dummy[:, 0:1], old2, new3)
                    for k2 in range(2):
                        LOADC(P, stage, wo[:, 4 * k2:4 * k2 + 4, :], wo_t[:, 4 * k2:4 * k2 + 4, :], (4, 1024),
                              [(N("wo"), k2)])

                    def load_dm(dm):
                        i = dm % 2
                        LOADC(P, stage, wm[i][:], wm_t[dm], (3, 8, 128), [N("wm%d" % i)])
                        LOADC(P, stage, wbr[i][:], wb_t[dm], (2, 4, 128), [N("wbr%d" % i)])
                        LOADC(P, stage, wb2[i][:], wb2_t[dm], (8, 128), [N("wb2%d" % i)], np_=64)

                    load_dm(0)
                    for dm in range(8):
                        if dm + 1 < 8:
                            load_dm(dm + 1)
                        i = dm % 2
                        for tt in range(2):
                            for n in range(3):
                                bm = nb()
                                for kc in range(8):
                                    MM(P, ps[bm][:], wm[i][:, n, kc, :], xb[:, kc, tt, :], kc == 0, kc == 7,
                                       [N("wm%d" % i), (N("xb"), tt)], [PK(bm)])
                                t_, tk = nt()
                                ACT(P, t_[:], ps[bm][:], AF.Sigmoid, [PK(bm)], [tk])
                                bb = 2 + (n % 2)
                                if n < 2:
                                    ysrc, yk = (y_a, N("y_a")) if n == 0 else (y_b, N("y_b"))
                                    for kc in range(4):
                                        MM(P, ps[bb][:], wbr[i][:, n, kc, :], ysrc[:, kc, tt, :], kc == 0, kc == 3,
                                           [N("wbr%d" % i), (yk, kc, tt)], [PK(bb)])
                                else:
                                    for h in range(8):
                                        MM(P, ps[bb][:], wb2[i][:, h, :], y_c[:, h, tt, :], h == 0, h == 7,
                                           [N("wb2%d" % i), (N("y_c"), h, tt)], [PK(bb)])
                                if n == 0:
                                    TT(P, "dve", mix[:], t_[:], ps[bb][:], ALU.mult, [tk, PK(bb)], [N("mix")])
                                elif n == 1:
                                    TT(P, "dve", t2[:], t_[:], ps[bb][:], ALU.mult, [tk, PK(bb)], [N("t2")])
                                    TT(P, "dve", mix[:], mix[:], t2[:], ALU.add, [N("mix"), N("t2")], [N("mix")])
                                else:
                                    TT(P, "dve", t2[:], t_[:], ps[bb][:], ALU.mult, [tk, PK(bb)], [N("t2")])
                                    TT(P, "dve", mixb[:, dm, tt, :], mix[:], t2[:], ALU.add, [N("mix"), N("t2")],
                                       [(N("mixb"), dm, tt)])
                    if STOP == "mrg":
                        P.emit()
                        return nc
                    for tt in range(2):
                        c0 = tl[tt] * 512
                        DMA(P, "sp", xres[:], xT[:, :, c0:c0 + 512].rearrange("k p c -> p k c"), [], [N("xres")],
                            N("xres"))
                        for dmo in range(8):
                            b = 4 + (dmo % 2)
                            for kc in range(8):
                                MM(P, ps[b][:], wo[:, kc, dmo * 128:(dmo + 1) * 128], mixb[:, kc, tt, :], kc == 0, kc == 7,
                                   [N("wo"), (N("mixb"), kc, tt)], [PK(b)])
                            STT(P, "dve", rr[:, dmo, :], xres[:, dmo, :], ALPHA, ps[b][:], ALU.mult, ALU.add,
                                [N("xres"), PK(b)], [(N("rr"), dmo)])
                            t_, tk = nt()
                            ACT(P, t_[:], rr[:, dmo, :], AF.Square, [(N("rr"), dmo)], [tk])
                            MM(P, ps[6][:], onesF[:], rr[:, dmo, :], dmo == 0, dmo == 7, ["onesF", (N("rr"), dmo)],
                               [PK(6)])
                            MM(P, ps[7][:], onesF[:], t_[:], dmo == 0, dmo == 7, ["onesF", tk], [PK(7)])
                        TS(P, "dve", mS[:], ps[6][:], 1.0 / 1024, None, ALU.mult, None, [PK(6)], [N("mS3")])
                        TT(P, "dve", t2[:], mS[:], mS[:], ALU.mult, [N("mS3")], [N("t2")])
                        STT(P, "dve", rS[:], ps[7][:], 1.0 / 1024, t2[:], ALU.mult, ALU.subtract, [PK(7), N("t2")],
                            [N("rS3")])
                        ACT(P, rS[:], rS[:], AF.Sqrt, [N("rS3"), "epsc"], [N("rS3")], bias=epsc[:])
                        OP1(P, "dve", "reciprocal", rS[:], rS[:], [N("rS3")], [N("rS3")])
                        for dmo in range(8):
                            TT(P, "dve", rr[:, dmo, :], rr[:, dmo, :], mS[:], ALU.subtract, [(N("rr"), dmo), N("mS3")],
                               [(N("rr"), dmo)])
                            TT(P, "dve", rr[:, dmo, :], rr[:, dmo, :], rS[:], ALU.mult, [(N("rr"), dmo), N("rS3")],
                               [(N("rr"), dmo)])
                            ACT(P, rr[:, dmo, :], rr[:, dmo, :], AF.Identity, [(N("rr"), dmo), "lnp_s"], [(N("rr"), dmo)],
                                bias=lnp_s[:, 1, dmo:dmo + 1], scale=lnp_s[:, 0, dmo:dmo + 1])
                            DMA(P, "sp", xT_out[dmo, :, c0:c0 + 512], rr[:, dmo, :], [(N("rr"), dmo)],
                                ["xT_out_%d_%d" % (dmo, tl[tt])], "xout%d" % dmo)
                    old3 = new3
                old_glob = glob_new
            FENCE(P, dummy[:, 1:2], old3 + old_glob, ["passdone%d" % pidx])
        P.emit()
    return nc


_CACHE = {}


def _get(name, fn):
    if name not in _CACHE:
        _CACHE[name] = fn()
    return _CACHE[name]


def _consts():
    p = np.arange(128)
    identF = np.eye(128, dtype=np.float32)
    maskT = (p[:, None] <= p[None, :]).astype(np.float32)
    slopes = 2.0 ** (-(np.arange(1, 9, dtype=np.float64)))
    rel = np.arange(64)
    alibi = (slopes[None, :, None] * (128.0 * (3 - rel)[None, None, :] + p[:, None, None])).astype(np.float32)
    eauxk = np.zeros((32, 64, 128), ml_dtypes.bfloat16)
    for sl in range(64):
        eauxk[sl // 2, sl, :] = 1.0
    eauxk = eauxk.reshape(32, 64 * 128)
    shiftc = np.zeros((128, 4, 8), np.float32)
    for ci in range(4):
        shiftc[:, ci, :] = -(slopes[None, :] * (128.0 * ci + p[:, None]))
    ones = np.ones((128, 128), np.float32)
    per_rank = []
    for r in range(4):
        cm = np.zeros((4, 2, 32), np.float32)
        fo = np.zeros((4, 2, 32), np.float32)
        for i, G in enumerate(GS[r]):
            for hf in range(2):
                jr_own = 1 - hf
                for jr in range(32):
                    elig = (jr > jr_own) and (jr <= 2 * G + 1)
                    if elig:
                        cm[i, hf, jr] = 0.0; fo[i, hf, jr] = 0.0
                    elif jr == jr_own:
                        cm[i, hf, jr] = -2e30; fo[i, hf, jr] = BIG
                    else:
                        cm[i, hf, jr] = -1e30; fo[i, hf, jr] = -BIG
        per_rank.append((np.broadcast_to(cm, (128, 4, 2, 32)).copy(), np.broadcast_to(fo, (128, 4, 2, 32)).copy()))
    return dict(identF=identF, maskT=maskT, alibi=alibi, eauxk=eauxk, shiftc=shiftc, onesF=ones), per_rank


def shard_x(x):
    xT = []
    for c in range(8):
        b, r = c // 4, c % 4
        cols = np.concatenate([x[b, G * 512:(G + 1) * 512, :] for G in GS[r]], axis=0)
        xT.append(np.ascontiguousarray(cols.T.reshape(8, 128, 2048)))
    return xT


def layer_weights(l, w_in, sg_w, sg_b, v_ln_g, v_ln_b, conv_w, conv_b, cv_ln_g, cv_ln_b, w_branch, w_out, ln_g, ln_b):
    f = np.float32
    wi = np.asarray(w_in[l], f)
    d = {}
    d["w_in_t"] = np.ascontiguousarray(wi.reshape(8, 128, 16, 512).transpose(2, 1, 0, 3))
    wmrg = wi[:, 5120:].reshape(8, 128, 3, 8, 128)
    d["wm_t"] = np.ascontiguousarray(wmrg.transpose(3, 1, 2, 0, 4))
    wb = np.asarray(w_branch[l], f)
    d["wb_t"] = np.ascontiguousarray(wb[0:2].reshape(2, 4, 128, 8, 128).transpose(3, 2, 0, 1, 4))
    d["wb2_t"] = np.ascontiguousarray(wb[2].reshape(8, 64, 8, 128).transpose(2, 1, 0, 3))
    d["wo_t"] = np.ascontiguousarray(np.asarray(w_out[l], f).reshape(8, 128, 1024).transpose(1, 0, 2))
    d["sgwT"] = np.ascontiguousarray(np.asarray(sg_w[l], f).transpose(2, 0, 1))
    d["sgb_bc"] = np.ascontiguousarray(np.broadcast_to(np.asarray(sg_b[l], f)[None], (128, 4, 128)))
    d["vg_bc"] = np.ascontiguousarray(np.broadcast_to(np.asarray(v_ln_g[l], f)[None], (128, 512)))
    d["vb_bc"] = np.ascontiguousarray(np.broadcast_to(np.asarray(v_ln_b[l], f)[None], (128, 512)))
    d["convwT"] = np.ascontiguousarray(np.asarray(conv_w[l], f).reshape(31, 4, 128).transpose(2, 1, 0))
    d["colp"] = np.ascontiguousarray(np.stack([np.asarray(a[l], f).reshape(4, 128).T
                                               for a in (conv_b, cv_ln_g, cv_ln_b)], axis=1))
    d["lnp"] = np.ascontiguousarray(np.stack([np.asarray(a[l], f).reshape(8, 128).T for a in (ln_g, ln_b)], axis=1))
    return d


def exchange(resA):
    f = np.float32
    outs = []
    for b in range(2):
        Kg = np.zeros((4, 128, 64, 128), ml_dtypes.bfloat16)
        Vg = np.zeros((4, 128, 64, 2, 80), ml_dtypes.bfloat16)
        kmg = np.zeros((128, 4, 32), f)
        tlg = np.zeros((128, 4, 16, 30), f)
        for r in range(4):
            ra = resA[b * 4 + r]
            Kl = np.asarray(ra["K_loc"]).reshape(4, 128, 4, 4, 128)
            Vl = np.asarray(ra["V_loc"]).reshape(4, 128, 4, 4, 2, 80)
            for i, G in enumerate(GS[r]):
                Kg[:, :, 4 * G:4 * G + 4, :] = Kl[:, :, i]
                Vg[:, :, 4 * G:4 * G + 4] = Vl[:, :, i]
                kmg[:, :, 2 * G:2 * G + 2] = np.asarray(ra["km_loc"])[:, :, 2 * i:2 * i + 2]
                tlg[:, :, G, :] = np.asarray(ra["tail_loc"])[:, :, i, :]
        Kh = Kg.reshape(4, 2, 64, 64, 128).reshape(8, 64, 64, 128)
        Vh = Vg.transpose(0, 3, 1, 2, 4).reshape(8, 128, 64, 80)
        kmh = kmg.reshape(2, 64, 4, 32).transpose(1, 2, 0, 3).reshape(64, 8, 32)
        for r in range(4):
            K_arr = np.zeros((8, 64, 160, 128), ml_dtypes.bfloat16)
            V_arr = np.zeros((8, 128, 160, 80), ml_dtypes.bfloat16)
            km_arr = np.zeros((64, 8, 4, 32), f)
            halo = np.zeros((128, 4, 4, 30), f)
            for i, G in enumerate(GS[r]):
                nv = 4 * G + 4
                idx = (4 * G + 3) - np.arange(nv)
                K_arr[:, :, SLOT_OFF[i]:SLOT_OFF[i] + nv] = Kh[:, :, idx]
                V_arr[:, :, SLOT_OFF[i]:SLOT_OFF[i] + nv] = Vh[:, :, idx]
                nj = 2 * G + 2
                km_arr[:, :, i, 0:nj] = kmh[:, :, (2 * G + 1) - np.arange(nj)]
                if G > 0:
                    halo[:, :, i, :] = tlg[:, :, G - 1, :]
            outs.append({"K_arr": K_arr.reshape(8, 64, 160 * 128), "V_arr": V_arr, "km_arr": km_arr, "halo": halo})
    return outs


def unshard(xT):
    out = np.zeros((2, 8192, 1024), np.float32)
    for c in range(8):
        b, r = c // 4, c % 4
        cols = np.asarray(xT[c], np.float32).reshape(1024, 2048).T
        for i, G in enumerate(GS[r]):
            out[b, G * 512:(G + 1) * 512, :] = cols[i * 512:(i + 1) * 512]
    return out


def kernel(x, w_in, sg_w, sg_b, v_ln_g, v_ln_b, conv_w, conv_b, cv_ln_g, cv_ln_b, w_branch, w_out, ln_g, ln_b):
    x = np.asarray(x, np.float32)
    consts, per_rank = _consts()
    ncA = _get("A", build_A)
    ncB = _get("B", build_B)
    xT = shard_x(x)
    for l in range(DEPTH):
        lw = layer_weights(l, w_in, sg_w, sg_b, v_ln_g, v_ln_b, conv_w, conv_b, cv_ln_g, cv_ln_b, w_branch, w_out,
                           ln_g, ln_b)
        resA = run_bass_kernel_spmd(ncA, [{"xT": xT[c], "w_in_t": lw["w_in_t"]} for c in range(8)],
                                    core_ids=list(range(8))).results
        ex = exchange(resA)
        ins = []
        for c in range(8):
            d = {"xT": xT[c], "cmask": per_rank[c % 4][0], "force": per_rank[c % 4][1]}
            d.update(lw)
            d.update(ex[c])
            d.update(consts)
            ins.append(d)
        resB = run_bass_kernel_spmd(ncB, ins, core_ids=list(range(8))).results
        xT = [np.asarray(resB[c]["xT_out"], np.float32) for c in range(8)]
    return unshard(xT)
```
